# Optimizing a Trainium2 kernel written in Bass

```python
import math
import jax, jax.numpy as jnp
from jax import lax
import numpy as np

D_MODEL = 2048
BATCH = 4
SEQ = 2048
DEPTH = 4

D_MIX = D_MODEL
D_MLSTM = D_MIX // 2
D_MOBA = D_MIX - D_MLSTM
MLSTM_HEADS = 4
MLSTM_HD = D_MLSTM // MLSTM_HEADS
MLSTM_CHUNK = 64
MLSTM_CONV = 4
MOBA_HEADS = 8
MOBA_HD = D_MOBA // MOBA_HEADS
MOBA_BLOCK = 256
MOBA_TOPK = 3
MOBA_QCHUNK = 32
N_BUCKETS = 32
MAX_DISTANCE = 2048
D_FF = 5632
FFN_CONV = 3
EPS = 1e-6
D_IN = 4 * D_MLSTM + 2 * MLSTM_HEADS + 3 * D_MOBA
MIX_SPLITS = (D_MLSTM, 2 * D_MLSTM, 3 * D_MLSTM, 4 * D_MLSTM,
              4 * D_MLSTM + MLSTM_HEADS, 4 * D_MLSTM + 2 * MLSTM_HEADS,
              4 * D_MLSTM + 2 * MLSTM_HEADS + D_MOBA, 4 * D_MLSTM + 2 * MLSTM_HEADS + 2 * D_MOBA)

kernel_name = "hybrid_mlstm_moba_convffn"


def rms_norm(x, g):
    xf = x.astype(jnp.float32)
    y = xf * lax.rsqrt(jnp.mean(xf * xf, axis=-1, keepdims=True) + EPS)
    return (y * g.astype(jnp.float32)).astype(x.dtype)


def causal_dwconv(x, w, b):
    K = w.shape[0]
    S = x.shape[1]
    xp = jnp.pad(x, ((0, 0), (K - 1, 0), (0, 0)))
    y = b
    for j in range(K):
        y = y + w[j] * xp[:, j:j + S]
    return y


def split_heads(t, n_heads):
    B, S, _ = t.shape
    return t.reshape(B, S, n_heads, -1).transpose(0, 2, 1, 3)


def merge_heads(t):
    B, H, S, dh = t.shape
    return t.transpose(0, 2, 1, 3).reshape(B, S, H * dh)


def t5_bucket(dist):
    max_exact = N_BUCKETS // 2
    d = jnp.maximum(dist, 0)
    ratio = jnp.maximum(d, max_exact).astype(jnp.float32) / max_exact
    large = max_exact + (jnp.log(ratio) / math.log(MAX_DISTANCE / max_exact)
                         * (N_BUCKETS - max_exact)).astype(jnp.int32)
    return jnp.where(d < max_exact, d, jnp.minimum(large, N_BUCKETS - 1))


def mlstm_chunkwise(q, k, v, log_i, log_f):
    B, H, S, dh = q.shape
    L = MLSTM_CHUNK
    nc = S // L
    f32 = jnp.float32

    def chunks(t):
        return jnp.moveaxis(t.astype(f32).reshape(B, H, nc, L, *t.shape[3:]), 2, 0)

    qc = chunks(q) * (dh ** -0.5)
    kc, vc, lic, lfc = chunks(k), chunks(v), chunks(log_i), chunks(log_f)
    causal = jnp.tril(jnp.ones((L, L), dtype=bool))

    def step(carry, inp):
        C, n, m = carry
        qt, kt, vt, li, lf = inp
        b = jnp.cumsum(lf, axis=-1)
        D = b[..., :, None] - b[..., None, :] + li[..., None, :]
        D = jnp.where(causal, D, -jnp.inf)
        inter = b + m[..., None]
        m_t = jnp.maximum(inter, jnp.max(D, axis=-1))
        w_inter = jnp.exp(inter - m_t)
        s = jnp.einsum('bhtd,bhsd->bhts', qt, kt) * jnp.exp(D - m_t[..., None])
        num = (w_inter[..., None] * jnp.einsum('bhtd,bhde->bhte', qt, C)
               + jnp.einsum('bhts,bhse->bhte', s, vt))
        den = w_inter * jnp.einsum('bhtd,bhd->bht', qt, n) + jnp.sum(s, axis=-1)
        h = num / jnp.maximum(jnp.abs(den), jnp.exp(-m_t))[..., None]
        bL = b[..., -1]
        g = bL[..., None] - b + li
        m_new = jnp.maximum(bL + m, jnp.max(g, axis=-1))
        a = jnp.exp(bL + m - m_new)
        wk = jnp.exp(g - m_new[..., None])
        C = a[..., None, None] * C + jnp.einsum('bhs,bhsd,bhse->bhde', wk, kt, vt)
        n = a[..., None] * n + jnp.einsum('bhs,bhsd->bhd', wk, kt)
        return (C, n, m_new), h

    init = (jnp.zeros((B, H, dh, dh), f32), jnp.zeros((B, H, dh), f32), jnp.zeros((B, H), f32))
    _, h = lax.scan(step, init, (qc, kc, vc, lic, lfc))
    h = jnp.moveaxis(h, 0, 2).reshape(B, H, S, dh)
    return h.astype(q.dtype)


def moba_attention(q, k, v, rel_bias):
    B, H, S, dh = q.shape
    f32 = jnp.float32
    nb = -(-S // MOBA_BLOCK)
    sp = nb * MOBA_BLOCK
    pad = ((0, 0), (0, 0), (0, sp - S), (0, 0))
    q, k, v = jnp.pad(q, pad), jnp.pad(k, pad), jnp.pad(v, pad)
    kb = k.reshape(B, H, nb, MOBA_BLOCK, dh)
    vb = v.reshape(B, H, nb, MOBA_BLOCK, dh)
    k_mean = jnp.mean(kb.astype(f32), axis=3)
    q_blk = jnp.arange(sp) // MOBA_BLOCK
    gate = jnp.einsum('bhsd,bhnd->bhsn', q.astype(f32), k_mean)
    past = jnp.arange(nb)[None, :] < q_blk[:, None]
    gate = jnp.where(past, gate, -jnp.inf)
    n_sel = min(MOBA_TOPK, nb)
    _, sel = lax.top_k(gate, n_sel)
    valid = sel < q_blk[:, None]
    nq = sp // MOBA_QCHUNK

    def to_chunks(t):
        return jnp.moveaxis(t.reshape(B, H, nq, MOBA_QCHUNK, *t.shape[3:]), 2, 0)

    bias_t = rel_bias.astype(f32).T
    b_idx = jnp.arange(B)[:, None, None, None]
    h_idx = jnp.arange(H)[None, :, None, None]
    offs = jnp.arange(MOBA_BLOCK)
    scale = dh ** -0.5
    n_cat = n_sel * MOBA_BLOCK

    def one_chunk(args):
        qc, selc, validc, c = args
        q_pos = c * MOBA_QCHUNK + jnp.arange(MOBA_QCHUNK)
        k_sel = kb[b_idx, h_idx, selc]
        v_sel = vb[b_idx, h_idx, selc]
        s_sel = jnp.einsum('bhqd,bhqnkd->bhqnk', qc, k_sel, preferred_element_type=f32) * scale
        k_pos = selc[..., None] * MOBA_BLOCK + offs
        s_sel = s_sel + bias_t[h_idx[..., None], t5_bucket(q_pos[:, None, None] - k_pos)]
        s_sel = jnp.where(validc[..., None], s_sel, -jnp.inf).reshape(B, H, MOBA_QCHUNK, n_cat)
        own = (c * MOBA_QCHUNK) // MOBA_BLOCK
        k_own = lax.dynamic_index_in_dim(kb, own, axis=2, keepdims=False)
        v_own = lax.dynamic_index_in_dim(vb, own, axis=2, keepdims=False)
        rel = q_pos[:, None] - (own * MOBA_BLOCK + offs)[None, :]
        s_own = (jnp.einsum('bhqd,bhkd->bhqk', qc, k_own, preferred_element_type=f32) * scale
                 + bias_t[:, t5_bucket(rel)])
        s_own = jnp.where(rel >= 0, s_own, -jnp.inf)
        p = jax.nn.softmax(jnp.concatenate([s_sel, s_own], axis=-1), axis=-1).astype(v.dtype)
        p_sel = p[..., :n_cat].reshape(B, H, MOBA_QCHUNK, n_sel, MOBA_BLOCK)
        out = (jnp.einsum('bhqnk,bhqnkd->bhqd', p_sel, v_sel, preferred_element_type=f32)
               + jnp.einsum('bhqk,bhkd->bhqd', p[..., n_cat:], v_own, preferred_element_type=f32))
        return out.astype(q.dtype)

    out = lax.map(one_chunk, (to_chunks(q), to_chunks(sel), to_chunks(valid), jnp.arange(nq)))
    out = jnp.moveaxis(out, 0, 2).reshape(B, H, sp, dh)
    return out[:, :, :S]


def setup_inputs(seed: int = 0) -> dict:
    key = jax.random.key(seed)
    ks = jax.random.split(key, 18)

    def nrm(k, shape, scale):
        return jax.random.normal(k, shape, jnp.float32) * scale

    x = nrm(ks[0], (BATCH, SEQ, D_MODEL), 1.0)
    norm_mix = 1.0 + nrm(ks[1], (DEPTH, D_MODEL), 0.02)
    w_in = nrm(ks[2], (DEPTH, D_MODEL, D_IN), D_MODEL ** -0.5)
    gate_bias = jnp.concatenate(
        [nrm(ks[3], (DEPTH, MLSTM_HEADS), 0.1),
         jnp.linspace(3.0, 6.0, MLSTM_HEADS)[None, :] + nrm(ks[4], (DEPTH, MLSTM_HEADS), 0.1)], axis=-1)
    conv_qk_w = nrm(ks[5], (DEPTH, MLSTM_CONV, 2 * D_MLSTM), MLSTM_CONV ** -0.5)
    conv_qk_b = nrm(ks[6], (DEPTH, 2 * D_MLSTM), 0.02)
    mlstm_norm = 1.0 + nrm(ks[7], (DEPTH, MLSTM_HEADS, MLSTM_HD), 0.02)
    qk_norm = 1.0 + nrm(ks[8], (DEPTH, 2, MOBA_HD), 0.02)
    rel_bias = nrm(ks[9], (N_BUCKETS, MOBA_HEADS), 0.5)
    w_out = nrm(ks[10], (DEPTH, D_MIX, D_MODEL), D_MIX ** -0.5)
    norm_ffn = 1.0 + nrm(ks[11], (DEPTH, D_MODEL), 0.02)
    w_up = nrm(ks[12], (DEPTH, D_MODEL, 2 * D_FF), D_MODEL ** -0.5)
    conv_ffn_w = nrm(ks[13], (DEPTH, FFN_CONV, 2 * D_FF), FFN_CONV ** -0.5)
    conv_ffn_b = nrm(ks[14], (DEPTH, 2 * D_FF), 0.02)
    w_down = nrm(ks[15], (DEPTH, D_FF, D_MODEL), D_FF ** -0.5)
    return {"x": x, "norm_mix": norm_mix, "w_in": w_in, "gate_bias": gate_bias,
            "conv_qk_w": conv_qk_w, "conv_qk_b": conv_qk_b, "mlstm_norm": mlstm_norm,
            "qk_norm": qk_norm, "rel_bias": rel_bias, "w_out": w_out, "norm_ffn": norm_ffn,
            "w_up": w_up, "conv_ffn_w": conv_ffn_w, "conv_ffn_b": conv_ffn_b, "w_down": w_down}


def reference(x, norm_mix, w_in, gate_bias, conv_qk_w, conv_qk_b, mlstm_norm, qk_norm, rel_bias,
              w_out, norm_ffn, w_up, conv_ffn_w, conv_ffn_b, w_down):
    H = MLSTM_HEADS
    for l in range(DEPTH):
        h = rms_norm(x, norm_mix[l])
        z = jnp.einsum('bsd,de->bse', h, w_in[l])
        mq, mk, mv, mo, gi, gf, aq, ak, av = jnp.split(z, MIX_SPLITS, axis=-1)
        qk = jax.nn.silu(causal_dwconv(jnp.concatenate([mq, mk], axis=-1), conv_qk_w[l], conv_qk_b[l]))
        mq, mk = qk[..., :D_MLSTM], qk[..., D_MLSTM:]
        gates = jnp.concatenate([gi, gf], axis=-1).astype(jnp.float32) + gate_bias[l]
        log_i = gates[..., :H].transpose(0, 2, 1)
        log_f = jax.nn.log_sigmoid(gates[..., H:]).transpose(0, 2, 1)
        hm = mlstm_chunkwise(split_heads(mq, H), split_heads(mk, H), split_heads(mv, H), log_i, log_f)
        hm = rms_norm(hm, mlstm_norm[l][:, None, :])
        hm = merge_heads(hm) * jax.nn.sigmoid(mo)
        aqh = rms_norm(split_heads(aq, MOBA_HEADS), qk_norm[l, 0])
        akh = rms_norm(split_heads(ak, MOBA_HEADS), qk_norm[l, 1])
        ha = merge_heads(moba_attention(aqh, akh, split_heads(av, MOBA_HEADS), rel_bias))
        x = x + jnp.einsum('bse,ed->bsd', jnp.concatenate([hm, ha], axis=-1), w_out[l])
        h = rms_norm(x, norm_ffn[l])
        u = causal_dwconv(jnp.einsum('bsd,df->bsf', h, w_up[l]), conv_ffn_w[l], conv_ffn_b[l])
        g, val = u[..., :D_FF], u[..., D_FF:]
        x = x + jnp.einsum('bsf,fd->bsd', jax.nn.silu(g) * val, w_down[l])
    return x
```

```python
import numpy as np
from contextlib import ExitStack
import concourse.bass as bass
import concourse.mybir as mybir
from concourse.bass_utils import run_bass_kernel_spmd

F32 = mybir.dt.float32
BF16 = mybir.dt.bfloat16
AF = mybir.ActivationFunctionType
ALU = mybir.AluOpType
AX = mybir.AxisListType

ENGS = ("pe", "act", "dve", "pool", "sp")


class Op:
    __slots__ = ("eng", "fn", "deps", "dma", "sem", "val", "signaled", "waits", "n")

    def __init__(self, eng, fn, dma):
        self.eng = eng
        self.fn = fn
        self.dma = dma
        self.deps = []
        self.sem = None
        self.val = 0
        self.signaled = False
        self.waits = []


class Prog:
    def __init__(self, nc, stack):
        self.nc = nc
        self.stack = stack
        self.ops = {e: [] for e in ENGS}
        self.lastw = {}
        self.readers = {}
        self.dma_keys = {}
        self.nops = 0

    def sem(self, name):
        return self.stack.enter_context(self.nc.semaphore(name))

    def op(self, eng, fn, r=(), w=(), dma=None):
        o = Op(eng, fn, dma)
        if dma is not None:
            o.signaled = True
        o.n = self.nops
        self.nops += 1
        deps = {}
        for k in r:
            lw = self.lastw.get(k)
            if lw is not None:
                deps[id(lw)] = lw
        for k in w:
            lw = self.lastw.get(k)
            if lw is not None:
                deps[id(lw)] = lw
            for rd in self.readers.get(k, {}).values():
                deps[id(rd)] = rd
        for d in deps.values():
            if d is o:
                continue
            if d.eng == "pe" and eng == "pe" and d.dma is None and dma is None:
                continue
            d.signaled = True
            o.deps.append(d)
        rk = eng if dma is None else ("dma", dma)
        for k in r:
            self.readers.setdefault(k, {})[rk] = o
        for k in w:
            self.lastw[k] = o
            self.readers[k] = {}
        self.ops[eng].append(o)
        return o

    def barrier(self):
        lasts = []
        for e in ENGS:
            seen_c = False
            seen_d = set()
            for o in reversed(self.ops[e]):
                if o.fn is None:
                    break
                if o.dma is None:
                    if not seen_c:
                        seen_c = True
                        lasts.append(o)
                else:
                    if o.dma not in seen_d:
                        seen_d.add(o.dma)
                        lasts.append(o)
        for e in ENGS:
            b = Op(e, None, None)
            b.n = self.nops
            self.nops += 1
            for d in lasts:
                d.signaled = True
                b.deps.append(d)
            self.ops[e].append(b)
        self.lastw = {}
        self.readers = {}

    def emit(self):
        nc = self.nc
        eng_sem = {e: self.sem("c_" + e) for e in ENGS}
        cnt = {e: 0 for e in ENGS}
        dsem = {}
        dcnt = {}
        allops = sorted((o for e in ENGS for o in self.ops[e]), key=lambda o: o.n)
        for o in allops:
            if not o.signaled:
                continue
            if o.dma is not None:
                if o.dma not in dsem:
                    dsem[o.dma] = self.sem("d_%s" % (o.dma,))
                    dcnt[o.dma] = 0
                dcnt[o.dma] += 16
                o.sem = dsem[o.dma]
                o.val = dcnt[o.dma]
            else:
                cnt[o.eng] += 1
                o.sem = eng_sem[o.eng]
                o.val = cnt[o.eng]
        handles = {"pe": nc.tensor, "act": nc.scalar, "dve": nc.vector,
                   "pool": nc.gpsimd, "sp": nc.sync}
        self.maxval = dict(cnt)
        self.ndsem = len(dsem)
        with nc.Block() as block:
            def make(e):
                def body(eng):
                    waited = {}
                    for o in self.ops[e]:
                        for d in o.deps:
                            key = id(d.sem)
                            if waited.get(key, 0) >= d.val:
                                continue
                            waited[key] = d.val
                            eng.wait_ge(d.sem, d.val)
                        if o.fn is None:
                            continue
                        ins = o.fn(eng)
                        if o.signaled:
                            ins.then_inc(o.sem, 16 if o.dma is not None else 1)
                return body
            block.tensor(make("pe"))
            block.scalar(make("act"))
            block.vector(make("dve"))
            block.gpsimd(make("pool"))
            block.sync(make("sp"))


D = 2048
DIN = 7176
DFF = 5632
NBUCK = 32
EPS = 1e-6
KCH = D // 128
NEG = -30000.0
FOFF = 638


def t5_bucket_np(d):
    d = np.maximum(d, 0)
    ratio = np.maximum(d, 16).astype(np.float32) / np.float32(16)
    large = 16 + (np.log(ratio) / np.float32(np.log(128.0)) * 16).astype(np.int32)
    return np.where(d < 16, d, np.minimum(large, 31))


def host_consts(S):
    NT = S // 128
    NB = S // 256
    W2 = S + FOFF + 2
    c = {}
    c["ident_f"] = np.eye(128, dtype=np.float32)
    s = np.arange(128)
    c["tri_f"] = (s[:, None] <= s[None, :]).astype(np.float32)
    c["ones_f"] = np.ones((128, 128), np.float32)
    oh = np.zeros((33, W2), np.float32)
    i = np.arange(W2)
    dist = i - FOFF
    b = t5_bucket_np(dist)
    for k in range(W2):
        if dist[k] >= 0:
            oh[b[k], k] = 1.0
        else:
            oh[32, k] = 1.0
    c["oh"] = oh
    qb = (np.arange(NT) // 2)
    j = np.arange(NB)
    past = (j[None, :] < qb[:, None])
    c["pastadd"] = np.where(past, 0.0, -1e30).astype(np.float32).reshape(1, NT * NB)
    c["pastneg"] = np.where(past, NEG, 0.0).astype(np.float32).reshape(1, NT * NB)
    e = np.zeros((8, NB, 128), np.float32)
    for jj in range(NB):
        e[jj, jj, :] = 1.0
    c["eall"] = e.reshape(8, NB * 128)
    return c


class Arena:
    def __init__(self, nc, lo, hi):
        self.nc, self.lo, self.hi, self.cur, self.n = nc, lo, hi, lo, 0

    def alloc(self, name, shape, dt):
        nb = 4 if dt == F32 else 2
        for d in shape[1:]:
            nb *= d
        off = (self.cur + 63) // 64 * 64
        assert off + nb <= self.hi, "SBUF arena overflow: %s needs %d at %d (hi %d)" % (name, nb, off, self.hi)
        self.cur = off + nb
        self.n += 1
        return self.nc.alloc_sbuf_tensor_at("%s_%d" % (name, self.n), list(shape), dt, offset=off)

    def mark(self):
        return self.cur

    def reset(self, m):
        self.cur = m


def build_program(S=2048, L=4, dbg=None):
    dbg = dbg or {}
    NT = S // 128
    NG = S // 512
    NB = S // 256
    W2 = S + FOFF + 2
    MW = S + 511
    nc = bass.Bass("TRN2", target_bir_lowering=False)

    def din(name, shape):
        return nc.dram_tensor(name, list(shape), F32, kind="ExternalInput").ap()

    x_in = din("x", [S, D])
    norm_mix = din("norm_mix", [L, D])
    w_in = din("w_in", [L, D, DIN])
    gate_bias = din("gate_bias", [L, 8])
    conv_qk_w = din("conv_qk_w", [L, 4, 2048])
    conv_qk_b = din("conv_qk_b", [L, 2048])
    mlstm_norm = din("mlstm_norm", [L, 1024])
    qk_norm = din("qk_norm", [L, 256])
    rel_bias = din("rel_bias", [32, 8])
    w_out = din("w_out", [L, D, D])
    norm_ffn = din("norm_ffn", [L, D])
    w_up = din("w_up", [L, D, 2 * DFF])
    conv_ffn_w = din("conv_ffn_w", [L, 3, 2 * DFF])
    conv_ffn_b = din("conv_ffn_b", [L, 2 * DFF])
    w_down = din("w_down", [L, DFF, D])
    c_ident = din("c_ident_f", [128, 128])
    c_tri = din("c_tri_f", [128, 128])
    c_ones = din("c_ones_f", [128, 128])
    c_oh = din("c_oh", [33, W2])
    c_pastadd = din("c_pastadd", [1, NT * NB])
    c_pastneg = din("c_pastneg", [1, NT * NB])
    c_eall = din("c_eall", [8, NB * 128])
    y = nc.dram_tensor("y", [S, D], F32, kind="ExternalOutput").ap()

    skind = "ExternalOutput" if dbg.get("scratch") else "Internal"
    qk_fm = nc.dram_tensor("qk_fm", [2048, S], BF16, kind=skind).ap()
    v_tm = nc.dram_tensor("v_tm", [S, 1024], BF16, kind=skind).ap()
    mo_tm = nc.dram_tensor("mo_tm", [S, 1024], BF16, kind=skind).ap()
    aqk_fm = nc.dram_tensor("aqk_fm", [2048, S], BF16, kind=skind).ap()
    av_tm = nc.dram_tensor("av_tm", [S, 1024], BF16, kind=skind).ap()
    a_fm = nc.dram_tensor("a_fm", [DFF, S], BF16, kind=skind).ap()
    d2 = nc.dram_tensor("d2", [8, 128, W2], F32, kind=skind).ap()
    dbg_outs = {}
    for name, shape in dbg.items():
        if isinstance(shape, (list, tuple)):
            dbg_outs[name] = nc.dram_tensor("dbg_" + name, list(shape), F32, kind="ExternalOutput").ap()

    def bcast_rows(ap2d, row, n):
        return bass.AP(ap2d.tensor, row * ap2d.shape[1], [[0, 128], [1, n]])

    stop = dbg.get("stop")
    with ExitStack() as top:
        P = Prog(nc, top)
        AR = Arena(nc, 16512, 229344)
        pbank = [nc.alloc_psum_tensor("pb%d" % b, [128, 512], F32) for b in range(8)]

        def PB(b):
            return ("pb", b)

        def pb16(b):
            return pbank[b][:].bitcast(BF16)

        ident_f = AR.alloc("ident_f", [128, 128], F32)
        ident_b = AR.alloc("ident_b", [128, 128], BF16)
        tri_f = AR.alloc("tri_f", [128, 128], F32)
        ones_f = AR.alloc("ones_f", [128, 128], F32)
        eall = AR.alloc("eall", [8, NB * 128], BF16)
        pastadd = AR.alloc("pastadd", [128, NT * NB], F32)
        pastneg = AR.alloc("pastneg", [128, NT * NB], F32)
        gates = AR.alloc("gates", [128, NT, 8], F32)
        epsc = AR.alloc("epsc", [128, 4], F32)
        P.op("pool", lambda e: e.memset(epsc[:, 0:1], D * EPS), w=["epsc"])
        P.op("pool", lambda e: e.memset(epsc[:, 1:2], 128.0 * EPS), w=["epsc"])
        P.op("pool", lambda e: e.memset(epsc[:, 2:3], 256.0 * EPS), w=["epsc"])
        P.op("pool", lambda e: e.memset(epsc[:, 3:4], 1.0), w=["epsc"])
        P.op("sp", lambda e: e.dma_start(out=ident_f[:], in_=c_ident), w=["ident_f"], dma="c0")
        P.op("sp", lambda e: e.dma_start(out=tri_f[:], in_=c_tri), w=["tri_f"], dma="c1")
        P.op("sp", lambda e: e.dma_start(out=ones_f[:], in_=c_ones), w=["ones_f"], dma="c2")
        P.op("pool", lambda e: e.dma_start(out=eall[:], in_=c_eall), w=["eall"], dma="c3")
        P.op("sp", lambda e: e.dma_start(out=pastadd[:], in_=bcast_rows(c_pastadd, 0, NT * NB)), w=["pastadd"], dma="c4")
        P.op("sp", lambda e: e.dma_start(out=pastneg[:], in_=bcast_rows(c_pastneg, 0, NT * NB)), w=["pastneg"], dma="c5")
        P.op("dve", lambda e: e.tensor_copy(out=ident_b[:], in_=ident_f[:]), r=["ident_f"], w=["ident_b"])
        PH = AR.mark()

        oh_sb = AR.alloc("oh_sb", [33, W2], F32)
        rb = AR.alloc("rb", [33, 8], F32)
        rbrep = AR.alloc("rbrep", [33, 8, 128], F32)
        frep = [AR.alloc("frep", [128, W2], F32) for i in range(2)]
        P.op("sp", lambda e: e.dma_start(out=oh_sb[:], in_=c_oh), w=["oh_sb"], dma="t0")
        P.op("pool", lambda e: e.memset(rb[:], -200.0), w=["rb"])
        P.op("sp", lambda e: e.dma_start(out=rb[0:32, :], in_=rel_bias), w=["rb"], dma="t1")
        for h in range(8):
            P.op("dve", lambda e, h=h: e.tensor_copy(out=rbrep[:, h, :], in_=rb[:, h:h + 1].to_broadcast([33, 128])),
                 r=["rb"], w=[("rbrep", h)])
        nck = (W2 + 511) // 512
        for h in range(8):
            fr = frep[h % 2]
            for ci in range(nck):
                c0 = ci * 512
                cw = min(512, W2 - c0)
                bk = ci % 2
                P.op("pe", lambda e, h=h, c0=c0, cw=cw, bk=bk: e.matmul(
                    pbank[bk][:, 0:cw], lhsT=rbrep[:, h, :], rhs=oh_sb[:, c0:c0 + cw], start=True, stop=True),
                    r=[("rbrep", h), "oh_sb"], w=[PB(bk)])
                P.op("act", lambda e, fr=fr, c0=c0, cw=cw, bk=bk: e.activation(
                    out=fr[:, c0:c0 + cw], in_=pbank[bk][:, 0:cw], func=AF.Exp),
                    r=[PB(bk)], w=[("frep", h % 2)])
            P.op("sp", lambda e, h=h, fr=fr: e.dma_start(out=d2[h], in_=fr[:]),
                 r=[("frep", h % 2)], w=[("d2", h)], dma="fst%d" % (h % 2))
        P.barrier()
        AR.reset(PH)

        def norm_transpose(src_ap, gvec_ap, lrow, hT, tag):
            m0 = AR.mark()
            gb = AR.alloc(tag + "gb", [128, D], F32)
            NXB = 4
            xt = [AR.alloc(tag + "xt", [128, D], F32) for i in range(NXB)]
            junk = AR.alloc(tag + "junk", [128, D], BF16)
            hb = [AR.alloc(tag + "hb", [128, D], BF16) for i in range(NXB)]
            ss = [AR.alloc(tag + "ss", [128, 4], F32) for i in range(NXB)]
            P.op("sp", lambda e: e.dma_start(out=gb[:], in_=bcast_rows(gvec_ap, lrow, D)), w=[tag + "gb"], dma=tag + "gb")
            P.op("dve", lambda e: e.tensor_scalar(out=gb[:], in0=gb[:], scalar1=float(D) ** 0.5, scalar2=None, op0=ALU.mult),
                 r=[tag + "gb"], w=[tag + "gb"])
            def n_stage1(t):
                s = t % NXB
                P.op("sp" if t % 2 == 0 else "act", lambda e: e.dma_start(out=xt[s][:], in_=src_ap[t * 128:(t + 1) * 128, :]),
                     w=[(tag + "xt", s)], dma=tag + "xt%d" % s)
                P.op("act", lambda e: e.activation(out=junk[:], in_=xt[s][:], func=AF.Square, accum_out=ss[s][:, 0:1]),
                     r=[(tag + "xt", s)], w=[tag + "junk", (tag + "ss", s)])
                P.op("act", lambda e: e.activation(out=ss[s][:, 1:2], in_=ss[s][:, 0:1], func=AF.Sqrt, bias=epsc[:, 0:1]),
                     r=[(tag + "ss", s), "epsc"], w=[(tag + "ss1", s)])
                P.op("dve", lambda e: e.reciprocal(out=ss[s][:, 2:3], in_=ss[s][:, 1:2]),
                     r=[(tag + "ss1", s)], w=[(tag + "ss2", s)])
                P.op("dve", lambda e: e.scalar_tensor_tensor(out=hb[s][:], in0=xt[s][:], scalar=ss[s][:, 2:3], in1=gb[:],
                                                             op0=ALU.mult, op1=ALU.mult),
                     r=[(tag + "xt", s), (tag + "ss2", s), tag + "gb"], w=[(tag + "hb", s)])

            def n_stage2(t):
                s = t % NXB
                for k4 in range(KCH // 4):
                    bk = 4 + k4 % 4
                    pv = pb16(bk)[:, 0:512].rearrange("p (a b) -> p a b", b=128)
                    for kk in range(4):
                        k = k4 * 4 + kk
                        P.op("pe", lambda e, k=k, kk=kk, pv=pv: e.transpose(
                            pv[:, kk, :], hb[s][:, k * 128:(k + 1) * 128], ident_b[:]),
                            r=[(tag + "hb", s), "ident_b"], w=[PB(bk)])
                    if k4 % 2 == 0:
                        P.op("act", lambda e, k4=k4, pv=pv: e.copy(
                            out=hT[:, k4 * 4:(k4 + 1) * 4, t * 128:(t + 1) * 128], in_=pv),
                            r=[PB(bk)], w=[("hT", t)])
                    else:
                        P.op("dve", lambda e, k4=k4, pv=pv: e.tensor_copy(
                            out=hT[:, k4 * 4:(k4 + 1) * 4, t * 128:(t + 1) * 128], in_=pv),
                            r=[PB(bk)], w=[("hT", t)])

            NLOOK = 2
            for t in range(NT + NLOOK):
                if t < NT:
                    n_stage1(t)
                if t - NLOOK >= 0:
                    n_stage2(t - NLOOK)
            P.barrier()
            AR.reset(m0)

        def load_wblock(wslots, idx, src3, c0, cw, tag, kn=KCH, nsplit=2):
            s = idx % len(wslots)
            step = (kn + nsplit - 1) // nsplit
            for hh in range(nsplit):
                k0, k1 = hh * step, min(kn, (hh + 1) * step)
                P.op("pool", lambda e, s=s, k0=k0, k1=k1: e.dma_start(
                    out=wslots[s][:, k0:k1, 0:cw], in_=src3[:, k0:k1, c0:c0 + cw]),
                    w=[(tag, s, hh)], dma="%s%d_%d" % (tag, s, hh))
            return s

        def wres(tag, s, nsplit=2):
            return [(tag, s, hh) for hh in range(nsplit)]

        for l in range(L if stop != "T" else 0):
            xsrc = x_in if l == 0 else y
            AR.reset(PH)
            hT = AR.alloc("hT", [128, KCH, S], BF16)
            norm_transpose(xsrc, norm_mix, l, hT, "n1")
            if stop == "A1":
                break
            win3 = w_in[l].rearrange("(k p) c -> p k c", p=128)
            wsl = [AR.alloc("wA", [128, KCH, 512], BF16) for i in range(2)]
            cw_sb = AR.alloc("cw_sb", [128, 4, 16], F32)
            cb_sb = AR.alloc("cb_sb", [128, 16], F32)
            gq = AR.alloc("gq", [128, 2], F32)
            gbias = AR.alloc("gbias", [128, 8], F32)
            u = [AR.alloc("u", [128, 3 + S], F32) for i in range(2)]
            acc = [AR.alloc("acc", [128, S], F32) for i in range(2)]
            sg = AR.alloc("sg", [128, S], F32)
            ofm = [AR.alloc("ofm", [128, S], BF16) for i in range(2)]
            otm = [AR.alloc("otm", [128, 512], BF16) for i in range(3)]
            zq = [AR.alloc("zq", [128, 512], F32) for i in range(2)]
            sq = [AR.alloc("sq", [128, 512], F32) for i in range(2)]
            rs = [AR.alloc("rs", [128, 512], F32) for i in range(2)]
            ps_g = pbank[7][:, 0:NT * 8].rearrange("p (t g) -> p t g", g=8)
            for tap in range(4):
                P.op("sp", lambda e, tap=tap, l=l: e.dma_start(out=cw_sb[:, tap, :], in_=conv_qk_w[l][tap].rearrange("(c p) -> p c", p=128),
                                                           allow_slow_non_contiguous=True), w=["cw_sb"], dma="cw")
            P.op("sp", lambda e, l=l: e.dma_start(out=cb_sb[:], in_=conv_qk_b[l].rearrange("(c p) -> p c", p=128),
                                              allow_slow_non_contiguous=True), w=["cb_sb"], dma="cb")
            P.op("dve", lambda e: e.tensor_scalar(out=cw_sb[:, :, 0:8], in0=cw_sb[:, :, 0:8], scalar1=1.0 / 16.0, scalar2=None,
                                                  op0=ALU.mult), r=["cw_sb"], w=["cw_sb"])
            P.op("dve", lambda e: e.tensor_scalar(out=cb_sb[:, 0:8], in0=cb_sb[:, 0:8], scalar1=1.0 / 16.0, scalar2=None,
                                                  op0=ALU.mult), r=["cb_sb"], w=["cb_sb"])
            P.op("sp", lambda e, l=l: e.dma_start(out=gq[:], in_=qk_norm[l].rearrange("(c p) -> p c", p=128),
                                              allow_slow_non_contiguous=True), w=["gq"], dma="gq")
            P.op("dve", lambda e: e.tensor_scalar(out=gq[:, 1:2], in0=gq[:, 1:2], scalar1=128.0 ** 0.5, scalar2=None,
                                                  op0=ALU.mult), r=["gq"], w=["gq"])
            P.op("sp", lambda e, l=l: e.dma_start(out=gbias[:], in_=bcast_rows(gate_bias, l, 8)), w=["gbias"], dma="gbias")
            for i in range(2):
                P.op("pool", lambda e, i=i: e.memset(u[i][:, 0:3], 0.0), w=[("u", i)])
            mcnt = [0]

            def mbank():
                b = mcnt[0] % 5
                mcnt[0] += 1
                return b

            blocks = [("q", i * 512, 512) for i in range(2)] + [("k", 1024 + i * 512, 512) for i in range(2)] + \
                     [("v", 2048 + i * 512, 512) for i in range(2)] + [("o", 3072 + i * 512, 512) for i in range(2)] + \
                     [("g", 4096, 8)] + [("aq", 4104 + i * 512, 512) for i in range(2)] + \
                     [("ak", 5128 + i * 512, 512) for i in range(2)] + [("av", 6152 + i * 512, 512) for i in range(2)]
            load_wblock(wsl, 0, win3, blocks[0][1], blocks[0][2], "wA")
            fmc = 0
            tmc = 0
            nq = 0
            pend_tail = []
            for bi, (kind, c0, cw) in enumerate(blocks):
                ws = bi % 2
                if kind == "av" and pend_tail:
                    pend_tail.pop()()
                if bi + 1 < len(blocks):
                    load_wblock(wsl, bi + 1, win3, blocks[bi + 1][1], blocks[bi + 1][2], "wA")
                wr = wres("wA", ws)
                if kind in ("q", "k"):
                    for j in range(4):
                        ch = (c0 + j * 128) // 128
                        us = fmc % 2
                        fmc += 1
                        for n in range(NG):
                            bk = mbank()
                            for k in range(KCH):
                                P.op("pe", lambda e, ws=ws, k=k, j=j, n=n, bk=bk: e.matmul(
                                    pbank[bk][:], lhsT=wsl[ws][:, k, j * 128:(j + 1) * 128],
                                    rhs=hT[:, k, n * 512:(n + 1) * 512], start=(k == 0), stop=(k == KCH - 1)),
                                    r=wr + ["hT"], w=[PB(bk)])
                            P.op("act", lambda e, us=us, n=n, bk=bk: e.copy(out=u[us][:, 3 + n * 512:3 + (n + 1) * 512],
                                                                             in_=pbank[bk][:]),
                                 r=[PB(bk)], w=[("u", us)])
                        ac = acc[us]
                        P.op("dve", lambda e, us=us, ch=ch, ac=ac: e.tensor_scalar(
                            out=ac[:], in0=u[us][:, 3:3 + S], scalar1=cw_sb[:, 3, ch:ch + 1], scalar2=cb_sb[:, ch:ch + 1],
                            op0=ALU.mult, op1=ALU.add), r=[("u", us), "cw_sb", "cb_sb"], w=[("acc", us)])
                        for tap in (2, 1, 0):
                            P.op("dve", lambda e, us=us, ch=ch, ac=ac, tap=tap: e.scalar_tensor_tensor(
                                out=ac[:], in0=u[us][:, tap:tap + S], scalar=cw_sb[:, tap, ch:ch + 1], in1=ac[:],
                                op0=ALU.mult, op1=ALU.add), r=[("u", us), ("acc", us), "cw_sb"], w=[("acc", us)])
                        if kind == "q":
                            P.op("act", lambda e, ac=ac: e.activation(out=sg[:], in_=ac[:], func=AF.Sigmoid, scale=16.0),
                                 r=[("acc", us)], w=["sg"])
                            P.op("pool", lambda e, ac=ac, us=us: e.tensor_tensor(out=ofm[us][:], in0=ac[:], in1=sg[:], op=ALU.mult),
                                 r=[("acc", us), "sg"], w=[("ofm", us)])
                        else:
                            P.op("act", lambda e, ac=ac, us=us: e.activation(out=ofm[us][:], in_=ac[:], func=AF.Silu),
                                 r=[("acc", us)], w=[("ofm", us)])
                        P.op("sp", lambda e, us=us, ch=ch: e.dma_start(out=qk_fm[ch * 128:(ch + 1) * 128, :], in_=ofm[us][:]),
                             r=[("ofm", us)], w=[("qk_fm", ch)], dma="ofm%d" % us)
                elif kind in ("v", "o", "av"):
                    dst = {"v": v_tm, "o": mo_tm, "av": av_tm}[kind]
                    cc0 = c0 - {"v": 2048, "o": 3072, "av": 6152}[kind]
                    for t in range(NT):
                        bk = mbank()
                        for k in range(KCH):
                            P.op("pe", lambda e, ws=ws, k=k, t=t, bk=bk: e.matmul(
                                pbank[bk][:], lhsT=hT[:, k, t * 128:(t + 1) * 128], rhs=wsl[ws][:, k, :],
                                start=(k == 0), stop=(k == KCH - 1)), r=wr + ["hT"], w=[PB(bk)])
                        os_ = tmc % 3
                        tmc += 1
                        if kind == "o":
                            P.op("act", lambda e, os_=os_, bk=bk: e.activation(out=otm[os_][:], in_=pbank[bk][:], func=AF.Sigmoid),
                                 r=[PB(bk)], w=[("otm", os_)])
                        else:
                            P.op("act", lambda e, os_=os_, bk=bk: e.copy(out=otm[os_][:], in_=pbank[bk][:]),
                                 r=[PB(bk)], w=[("otm", os_)])
                        P.op("sp", lambda e, os_=os_, t=t, dst=dst, cc0=cc0: e.dma_start(
                            out=dst[t * 128:(t + 1) * 128, cc0:cc0 + 512], in_=otm[os_][:]),
                            r=[("otm", os_)], w=[(kind, t, cc0)], dma="otm%d" % os_)
                elif kind == "g":
                    for t in range(NT):
                        for k in range(KCH):
                            P.op("pe", lambda e, ws=ws, k=k, t=t: e.matmul(
                                ps_g[:, t, :], lhsT=hT[:, k, t * 128:(t + 1) * 128], rhs=wsl[ws][:, k, 0:8],
                                start=(k == 0), stop=(k == KCH - 1)), r=wr + ["hT"], w=[PB(7)])
                    P.op("dve", lambda e: e.tensor_tensor(out=gates[:], in0=ps_g, in1=gbias[:].unsqueeze(1).to_broadcast([128, NT, 8]),
                                                          op=ALU.add), r=[PB(7), "gbias"], w=["gates"])
                else:
                    isq = kind == "aq"
                    for j in range(4):
                        hd = (c0 - (4104 if isq else 5128)) // 128 + j
                        fs = fmc % 2
                        fmc += 1
                        for n in range(NG):
                            bk = mbank()
                            z = nq % 2
                            nq += 1
                            for k in range(KCH):
                                P.op("pe", lambda e, ws=ws, k=k, j=j, n=n, bk=bk: e.matmul(
                                    pbank[bk][:], lhsT=wsl[ws][:, k, j * 128:(j + 1) * 128],
                                    rhs=hT[:, k, n * 512:(n + 1) * 512], start=(k == 0), stop=(k == KCH - 1)),
                                    r=wr + ["hT"], w=[PB(bk)])
                            P.op("act", lambda e, z=z, bk=bk: e.copy(out=zq[z][:], in_=pbank[bk][:]),
                                 r=[PB(bk)], w=[("zq", z)])
                            P.op("act", lambda e, z=z, bk=bk: e.activation(out=sq[z][:], in_=pbank[bk][:], func=AF.Square),
                                 r=[PB(bk)], w=[("sq", z)])
                            gcol = gq[:, 0:1] if isq else gq[:, 1:2]
                            row = (0 if isq else 1024) + hd * 128

                            def tail(z=z, fs=fs, n=n, gcol=gcol, row=row):
                                P.op("pe", lambda e: e.matmul(pbank[5 + z][:], lhsT=ones_f[:], rhs=sq[z][:], start=True, stop=True),
                                     r=[("sq", z), "ones_f"], w=[PB(5 + z)])
                                P.op("act", lambda e: e.activation(out=rs[z][:], in_=pbank[5 + z][:], func=AF.Sqrt, bias=epsc[:, 1:2]),
                                     r=[PB(5 + z), "epsc"], w=[("rs", z)])
                                P.op("dve", lambda e: e.reciprocal(out=rs[z][:], in_=rs[z][:]),
                                     r=[("rs", z)], w=[("rs", z)])
                                P.op("dve", lambda e: e.scalar_tensor_tensor(
                                    out=ofm[fs][:, n * 512:(n + 1) * 512], in0=zq[z][:], scalar=gcol, in1=rs[z][:],
                                    op0=ALU.mult, op1=ALU.mult), r=[("zq", z), ("rs", z), "gq"], w=[("ofm", fs)])
                                if n == NG - 1:
                                    P.op("sp", lambda e: e.dma_start(out=aqk_fm[row:row + 128, :], in_=ofm[fs][:]),
                                         r=[("ofm", fs)], w=[("aqk_fm", row)], dma="ofm%d" % fs)
                            if pend_tail:
                                pend_tail.pop()()
                            pend_tail.append(tail)
            if pend_tail:
                pend_tail.pop()()
            P.barrier()
            if stop == "A":
                break

            AR.reset(PH)
            mix = AR.alloc("mix", [128, NT, 2048], BF16)
            MIXM = AR.mark()
            gnb = AR.alloc("gnb", [128, 1024], F32)
            lfn = AR.alloc("lfn", [128, NT, 4], F32)
            t1 = AR.alloc("t1", [128, NT, 4], F32)
            e_s = AR.alloc("e_s", [128, NT, 4], F32)
            ebi = AR.alloc("ebi", [128, NT, 4], F32)
            ebL = AR.alloc("ebL", [128, NT, 4], F32)
            qF = [AR.alloc("qF", [128, 2, S], BF16) for i in range(2)]
            kF = [AR.alloc("kF", [128, 2, S], BF16) for i in range(2)]
            vA = [AR.alloc("vA", [128, NT, 257], BF16) for i in range(2)]
            moS = [AR.alloc("moS", [128, NT, 256], BF16) for i in range(2)]
            SdT = [AR.alloc("SdT", [128, 128], BF16) for i in range(2)]
            kw = [AR.alloc("kw", [128, 256], BF16) for i in range(2)]
            Cf = AR.alloc("Cf", [128, 2, 257], F32)
            Cb = [AR.alloc("Cb", [128, 2, 257], BF16) for i in range(2)]
            sm = [AR.alloc("sm", [128, 8], F32) for i in range(2)]
            junk2 = AR.alloc("junk2", [128, 256], BF16)
            hn = [AR.alloc("hn", [128, 256], F32) for i in range(2)]
            P.op("sp", lambda e, l=l: e.dma_start(out=gnb[:], in_=bcast_rows(mlstm_norm, l, 1024)), w=["gnb"], dma="gnb")
            for i in range(2):
                P.op("pool", lambda e, i=i: e.memset(vA[i][:, :, 256:257], 1.0), w=[("vA", i)])
            P.op("act", lambda e: e.activation(out=t1[:], in_=gates[:, :, 4:8], func=AF.Exp, scale=-1.0), r=["gates"], w=["t1"])
            P.op("act", lambda e: e.activation(out=lfn[:], in_=t1[:], func=AF.Ln, bias=epsc[:, 3:4]), r=["t1", "epsc"], w=["lfn"])
            bn_ps = pbank[7][:, 0:NT * 4].rearrange("p (t g) -> p t g", g=4)
            bL_ps = pbank[6][:, 0:NT * 4].rearrange("p (t g) -> p t g", g=4)
            for t in range(NT):
                P.op("pe", lambda e, t=t: e.matmul(bn_ps[:, t, :], lhsT=tri_f[:], rhs=lfn[:, t, :], start=True, stop=True),
                     r=["lfn", "tri_f"], w=[PB(7)])
                P.op("pe", lambda e, t=t: e.matmul(bL_ps[:, t, :], lhsT=ones_f[:], rhs=lfn[:, t, :], start=True, stop=True),
                     r=["lfn", "ones_f"], w=[PB(6)])
            P.op("dve", lambda e: e.tensor_tensor(out=t1[:], in0=gates[:, :, 0:4], in1=bn_ps, op=ALU.add), r=["gates", PB(7), "t1"], w=["t1"])
            P.op("act", lambda e: e.activation(out=e_s[:], in_=t1[:], func=AF.Exp), r=["t1"], w=["e_s"])
            P.op("act", lambda e: e.activation(out=ebi[:], in_=bn_ps, func=AF.Exp), r=[PB(7)], w=["ebi"])
            P.op("act", lambda e: e.activation(out=ebL[:], in_=bL_ps, func=AF.Exp, scale=-1.0), r=[PB(6)], w=["ebL"])
            for hd in range(4):
                hs = hd % 2
                for dc in range(2):
                    P.op("sp", lambda e, hs=hs, dc=dc, hd=hd: e.dma_start(
                        out=qF[hs][:, dc, :], in_=qk_fm[hd * 256 + dc * 128:hd * 256 + (dc + 1) * 128, :]),
                        w=[("qF", hs, dc)], dma="qF%d%d" % (hs, dc))
                    P.op("sp", lambda e, hs=hs, dc=dc, hd=hd: e.dma_start(
                        out=kF[hs][:, dc, :], in_=qk_fm[1024 + hd * 256 + dc * 128:1024 + hd * 256 + (dc + 1) * 128, :]),
                        w=[("kF", hs, dc)], dma="kF%d%d" % (hs, dc))
                P.op("sp", lambda e, hs=hs, hd=hd: e.dma_start(
                    out=vA[hs][:, :, 0:256], in_=v_tm[:, hd * 256:(hd + 1) * 256].rearrange("(t p) c -> p t c", p=128)),
                    w=[("vA", hs)], dma="vA%d" % hs)
                P.op("sp", lambda e, hs=hs, hd=hd: e.dma_start(
                    out=moS[hs][:], in_=mo_tm[:, hd * 256:(hd + 1) * 256].rearrange("(t p) c -> p t c", p=128)),
                    w=[("moS", hs)], dma="moS%d" % hs)
                qkr = [("qF", hs, 0), ("qF", hs, 1), ("kF", hs, 0), ("kF", hs, 1)]

                def front(c, hs=hs, hd=hd, qkr=qkr):
                    cs = slice(c * 128, (c + 1) * 128)
                    p2 = c % 2
                    for dc in range(2):
                        P.op("pe", lambda e, dc=dc: e.matmul(
                            pbank[p2][:, 0:128], lhsT=kF[hs][:, dc, cs], rhs=qF[hs][:, dc, cs], start=(dc == 0), stop=(dc == 1)),
                            r=qkr, w=[PB(p2)])
                    if c < NT - 1:
                        kTv = pb16(4)[:, 0:256]
                        for dc in range(2):
                            P.op("pe", lambda e, dc=dc: e.transpose(
                                kTv[:, dc * 128:(dc + 1) * 128], kF[hs][:, dc, cs], ident_b[:]),
                                r=qkr + ["ident_b"], w=[PB(4)])
                    P.op("dve", lambda e: e.scalar_tensor_tensor(
                        out=SdT[p2][:], in0=pbank[p2][:, 0:128], scalar=e_s[:, c, hd:hd + 1], in1=tri_f[:],
                        op0=ALU.mult, op1=ALU.mult), r=[PB(p2), "e_s", "tri_f"], w=[("SdT", p2)])
                    if c < NT - 1:
                        P.op("act", lambda e: e.activation(
                            out=kw[p2][:], in_=kTv, func=AF.Copy, scale=e_s[:, c, hd:hd + 1]),
                            r=[PB(4), "e_s"], w=[("kw", p2)])

                front(0)
                for c in range(NT):
                    cs = slice(c * 128, (c + 1) * 128)
                    p2 = c % 2
                    if c + 1 < NT:
                        front(c + 1)
                    if c < NT - 1:
                        for dc in range(2):
                            P.op("pe", lambda e, p2=p2, dc=dc, hs=hs, c=c: e.matmul(
                                pbank[5 + dc][:, 0:257], lhsT=kw[p2][:, dc * 128:(dc + 1) * 128], rhs=vA[hs][:, c, :],
                                start=True, stop=True), r=[("kw", p2), ("vA", hs)], w=[PB(5 + dc)])
                        np2 = (c + 1) % 2
                        if c == 0:
                            for dc in range(2):
                                P.op("act", lambda e, dc=dc, c=c, hd=hd: e.activation(
                                    out=Cf[:, dc, :], in_=pbank[5 + dc][:, 0:257], func=AF.Copy, scale=ebL[:, c, hd:hd + 1]),
                                    r=[PB(5 + dc), "ebL"], w=["Cf"])
                        else:
                            for dc in range(2):
                                P.op("dve", lambda e, dc=dc: e.tensor_tensor(
                                    out=Cf[:, dc, :], in0=Cf[:, dc, :], in1=pbank[5 + dc][:, 0:257], op=ALU.add),
                                    r=[PB(5 + dc), "Cf"], w=["Cf"])
                            P.op("act", lambda e, c=c, hd=hd: e.activation(
                                out=Cf[:], in_=Cf[:], func=AF.Copy, scale=ebL[:, c, hd:hd + 1]), r=["Cf", "ebL"], w=["Cf"])
                        P.op("pool", lambda e, np2=np2: e.tensor_copy(out=Cb[np2][:], in_=Cf[:]), r=["Cf"], w=[("Cb", np2)])
                    pn = pbank[2 + p2][:, 0:257]
                    if c > 0:
                        for dc in range(2):
                            P.op("pe", lambda e, hs=hs, dc=dc, cs=cs, p2=p2, pn=pn: e.matmul(
                                pn, lhsT=qF[hs][:, dc, cs], rhs=Cb[p2][:, dc, :], start=(dc == 0), stop=False),
                                r=qkr + [("Cb", p2)], w=[PB(2 + p2)])
                    P.op("pe", lambda e, hs=hs, c=c, p2=p2, pn=pn: e.matmul(
                        pn, lhsT=SdT[p2][:], rhs=vA[hs][:, c, :], start=(c == 0), stop=True),
                        r=[("SdT", p2), ("vA", hs)], w=[PB(2 + p2)])
                    smp = sm[p2]
                    P.op("dve", lambda e, smp=smp, pn=pn, c=c, hd=hd: e.tensor_tensor(
                        out=smp[:, 6:7], in0=pn[:, 256:257], in1=ebi[:, c, hd:hd + 1], op=ALU.max),
                        r=[PB(2 + p2), "ebi"], w=[("sm6", p2)])
                    P.op("dve", lambda e, smp=smp, pn=pn: e.scalar_tensor_tensor(
                        out=smp[:, 0:1], in0=pn[:, 256:257], scalar=-1.0, in1=smp[:, 6:7], op0=ALU.mult, op1=ALU.max),
                        r=[PB(2 + p2), ("sm6", p2)], w=[("sm0", p2)])
                    P.op("dve", lambda e, smp=smp: e.reciprocal(out=smp[:, 1:2], in_=smp[:, 0:1]), r=[("sm0", p2)], w=[("sm1", p2)])
                    P.op("act", lambda e, smp=smp, pn=pn: e.activation(
                        out=junk2[:], in_=pn[:, 0:256], func=AF.Square, scale=smp[:, 1:2], accum_out=smp[:, 2:3]),
                        r=[PB(2 + p2), ("sm1", p2)], w=["junk2", ("sm2", p2)])
                    P.op("act", lambda e, smp=smp: e.activation(out=smp[:, 3:4], in_=smp[:, 2:3], func=AF.Sqrt, bias=epsc[:, 2:3]),
                         r=[("sm2", p2), "epsc"], w=[("sm3", p2)])
                    P.op("dve", lambda e, smp=smp: e.reciprocal(out=smp[:, 4:5], in_=smp[:, 3:4]), r=[("sm3", p2)], w=[("sm4", p2)])
                    P.op("dve", lambda e, smp=smp: e.scalar_tensor_tensor(
                        out=smp[:, 5:6], in0=smp[:, 4:5], scalar=16.0, in1=smp[:, 1:2], op0=ALU.mult, op1=ALU.mult),
                        r=[("sm4", p2), ("sm1", p2)], w=[("sm5", p2)])
                    P.op("dve", lambda e, smp=smp, pn=pn, p2=p2, hd=hd: e.scalar_tensor_tensor(
                        out=hn[p2][:], in0=pn[:, 0:256], scalar=smp[:, 5:6], in1=gnb[:, hd * 256:(hd + 1) * 256],
                        op0=ALU.mult, op1=ALU.mult), r=[PB(2 + p2), ("sm5", p2), "gnb"], w=[("hn", p2)])
                    P.op("pool", lambda e, p2=p2, c=c, hd=hd, hs=hs: e.tensor_tensor(
                        out=mix[:, c, hd * 256:(hd + 1) * 256], in0=hn[p2][:], in1=moS[hs][:, c, :], op=ALU.mult),
                        r=[("hn", p2), ("moS", hs)], w=[("mix", c)])
            P.barrier()
            if "mix" in dbg_outs and stop == "B":
                P.op("pool", lambda e: e.dma_start(out=dbg_outs["mix"], in_=mix[:]), r=[("mix", c) for c in range(NT)], dma="dbgm")
            if stop == "B":
                break

            AR.reset(MIXM)
            qn = [AR.alloc("qn", [128, S], BF16) for i in range(2)]
            kn = [AR.alloc("kn", [128, S], BF16) for i in range(2)]
            vA2 = [AR.alloc("vA2", [128, NT, 129], BF16) for i in range(2)]
            Mt = [AR.alloc("Mt", [128, MW], F32) for i in range(2)]
            kmf = [AR.alloc("kmf", [128, NB], F32) for i in range(2)]
            kmb = [AR.alloc("kmb", [128, NB], BF16) for i in range(2)]
            gm = [AR.alloc("gm", [128, NT, NB], F32) for i in range(2)]
            top8 = [AR.alloc("top8", [128, 8], F32) for i in range(2)]
            negm = [AR.alloc("negm", [128, NT, 8], BF16) for i in range(2)]
            negT = [AR.alloc("negT", [8, S], BF16) for i in range(2)]
            pex = [AR.alloc("pex", [128, 512], F32) for i in range(3)]
            PT = [AR.alloc("PT", [128, 512], BF16) for i in range(3)]
            rcp = [AR.alloc("rcp", [128, 4], F32) for i in range(2)]
            for i in range(2):
                P.op("pool", lambda e, i=i: e.memset(vA2[i][:, :, 128:129], 1.0), w=[("vA2", i)])
                if NB < 8:
                    P.op("pool", lambda e, i=i: e.memset(negm[i][:], 0.0), w=[("negm", i)])
                    P.op("pool", lambda e, i=i: e.memset(gm[i][:], 0.0), w=[("gm", i)])
            pa3 = pastadd[:].rearrange("p (t j) -> p t j", j=NB)
            pn3 = pastneg[:].rearrange("p (t j) -> p t j", j=NB)

            def prep_head(hd):
                hs = hd % 2
                P.op("sp", lambda e: e.dma_start(out=qn[hs][:], in_=aqk_fm[hd * 128:(hd + 1) * 128, :]),
                     w=[("qn", hs)], dma="qn%d" % hs)
                P.op("sp", lambda e: e.dma_start(out=kn[hs][:], in_=aqk_fm[1024 + hd * 128:1024 + (hd + 1) * 128, :]),
                     w=[("kn", hs)], dma="kn%d" % hs)
                P.op("sp", lambda e: e.dma_start(
                    out=vA2[hs][:, :, 0:128], in_=av_tm[:, hd * 128:(hd + 1) * 128].rearrange("(t p) c -> p t c", p=128)),
                    w=[("vA2", hs)], dma="vA2%d" % hs)
                P.op("sp", lambda e: e.dma_start(
                    out=Mt[hs][:], in_=bass.AP(d2.tensor, hd * 128 * W2 + 127, [[W2 - 1, 128], [1, MW]])),
                    w=[("Mt", hs)], dma="Mt%d" % hs)
                P.op("dve", lambda e: e.tensor_reduce(out=kmf[hs][:], in_=kn[hs][:].rearrange("p (j t) -> p j t", t=256),
                                                      axis=AX.X, op=ALU.add), r=[("kn", hs)], w=[("kmf", hs)])
                P.op("dve", lambda e: e.tensor_scalar(out=kmb[hs][:], in0=kmf[hs][:], scalar1=1.0 / 256.0, scalar2=None, op0=ALU.mult),
                     r=[("kmf", hs)], w=[("kmb", hs)])
                g_ps = pbank[7][:, 0:NT * NB].rearrange("p (t j) -> p t j", j=NB)
                for t in range(NT):
                    P.op("pe", lambda e, t=t: e.matmul(
                        g_ps[:, t, :], lhsT=qn[hs][:, t * 128:(t + 1) * 128], rhs=kmb[hs][:], start=True, stop=True),
                        r=[("qn", hs), ("kmb", hs)], w=[PB(7)])
                P.op("dve", lambda e: e.tensor_tensor(out=gm[hs][:, :, 0:NB], in0=g_ps, in1=pa3, op=ALU.add),
                     r=[PB(7), "pastadd"], w=[("gm", hs)])
                if NB >= 8:
                    for t in range(NT):
                        tp = t % 2
                        P.op("dve", lambda e, t=t, tp=tp: e.max(out=top8[tp][:], in_=gm[hs][:, t, :]), r=[("gm", hs)], w=[("top8", tp)])
                        P.op("dve", lambda e, t=t, tp=tp: e.scalar_tensor_tensor(
                            out=negm[hs][:, t, 0:NB], in0=gm[hs][:, t, :], scalar=top8[tp][:, 2:3], in1=pn3[:, t, :],
                            op0=ALU.is_lt, op1=ALU.mult), r=[("gm", hs), ("top8", tp), "pastneg"], w=[("negm", hs)])
                nT_ps = pb16(7)
                for t4 in range(NG):
                    for tt in range(4):
                        t = t4 * 4 + tt
                        P.op("pe", lambda e, t=t, tt=tt: e.transpose(
                            nT_ps[0:8, tt * 128:(tt + 1) * 128], negm[hs][:, t, :], ident_b[:]),
                            r=[("negm", hs), "ident_b"], w=[PB(7)])
                    P.op("act", lambda e, t4=t4: e.copy(out=negT[hs][:, t4 * 512:(t4 + 1) * 512], in_=nT_ps[0:8, 0:512]),
                         r=[PB(7)], w=[("negT", hs)])

            accb = [pbank[3 + qi][:, 0:129] for qi in range(4)]
            accres = [PB(3 + qi) for qi in range(4)]
            pairs = []
            for hd in range(8):
                for cq in range(NG):
                    nkt = min(4 * cq + 4, NT)
                    for kt in range(nkt):
                        pairs.append((hd, cq, kt, kt == nkt - 1))

            def stage1(i):
                hd, cq, kt, last = pairs[i]
                hs = hd % 2
                s3 = i % 3
                jb = kt // 2
                need_mask = (NB >= 8) and (2 * cq + 1 >= 4) and (jb < 2 * cq + 1)
                P.op("pe", lambda e: e.matmul(
                    pbank[s3][:], lhsT=kn[hs][:, kt * 128:(kt + 1) * 128], rhs=qn[hs][:, cq * 512:(cq + 1) * 512],
                    start=True, stop=not need_mask), r=[("kn", hs), ("qn", hs)], w=[PB(s3)])
                if need_mask:
                    P.op("pe", lambda e: e.matmul(
                        pbank[s3][:], lhsT=eall[:, jb * 128:(jb + 1) * 128], rhs=negT[hs][:, cq * 512:(cq + 1) * 512],
                        start=False, stop=True), r=["eall", ("negT", hs)], w=[PB(s3)])
                P.op("act", lambda e: e.activation(out=pex[s3][:], in_=pbank[s3][:], func=AF.Exp),
                     r=[PB(s3)], w=[("pex", s3)])
                j0 = cq * 512 - kt * 128 + 511
                P.op("dve", lambda e: e.tensor_tensor(
                    out=PT[s3][:], in0=pex[s3][:], in1=Mt[hs][:, j0:j0 + 512], op=ALU.mult),
                    r=[("pex", s3), ("Mt", hs)], w=[("PT", s3)])

            def stage2(i):
                hd, cq, kt, last = pairs[i]
                hs = hd % 2
                s3 = i % 3
                for qi in range(4):
                    t = 4 * cq + qi
                    if t < kt:
                        continue
                    P.op("pe", lambda e, qi=qi, t=t: e.matmul(
                        accb[qi], lhsT=PT[s3][:, qi * 128:(qi + 1) * 128], rhs=vA2[hs][:, kt, :],
                        start=(kt == 0), stop=(kt == t)), r=[("PT", s3), ("vA2", hs)], w=[accres[qi]])
                if last:
                    ap_ = (hd * NG + cq) % 2
                    for qi in range(4):
                        t = 4 * cq + qi
                        rc = rcp[ap_]
                        P.op("dve", lambda e, qi=qi, rc=rc: e.reciprocal(out=rc[:, qi:qi + 1], in_=accb[qi][:, 128:129]),
                             r=[accres[qi]], w=[("rcp", ap_, qi)])
                        P.op("act", lambda e, qi=qi, rc=rc, t=t: e.activation(
                            out=mix[:, t, 1024 + hd * 128:1024 + (hd + 1) * 128], in_=accb[qi][:, 0:128],
                            func=AF.Copy, scale=rc[:, qi:qi + 1]), r=[accres[qi], ("rcp", ap_, qi)], w=[("mix", t)])

            LOOK = 2
            npairs = len(pairs)
            per_head = npairs // 8
            if dbg.get("skip") == "C":
                npairs = -LOOK
            else:
                prep_head(0)
            for i in range(npairs + LOOK):
                if i < npairs:
                    hd_i = pairs[i][0]
                    if (i % per_head) == min(per_head // 2, per_head - 1) and hd_i + 1 < 8:
                        prep_head(hd_i + 1)
                    stage1(i)
                if i - LOOK >= 0:
                    stage2(i - LOOK)
            P.barrier()
            if "mix" in dbg_outs and stop == "C":
                P.op("pool", lambda e: e.dma_start(out=dbg_outs["mix"], in_=mix[:]), r=[("mix", c) for c in range(NT)], dma="dbgm")
            if stop == "C":
                break

            AR.reset(MIXM)
            mixT = AR.alloc("mixT", [128, KCH, S], BF16)
            wD = [AR.alloc("wD", [128, KCH, 512], BF16) for i in range(2)]
            xr = [AR.alloc("xr", [128, 512], F32) for i in range(3)]
            wo3 = w_out[l].rearrange("(k p) c -> p k c", p=128)
            load_wblock(wD, 0, wo3, 0, 512, "wD")
            tcn = 0
            for t in range(NT):
                for k4 in range(KCH // 4):
                    bk = 6 + tcn % 2
                    tcn += 1
                    pv = pb16(bk)[:, 0:512].rearrange("p (a b) -> p a b", b=128)
                    for kk in range(4):
                        k = k4 * 4 + kk
                        P.op("pe", lambda e, t=t, k=k, kk=kk, pv=pv: e.transpose(
                            pv[:, kk, :], mix[:, t, k * 128:(k + 1) * 128], ident_b[:]),
                            r=[("mix", t), "ident_b"], w=[PB(bk)])
                    eng = "act" if k4 % 2 == 0 else "dve"
                    if eng == "act":
                        P.op("act", lambda e, t=t, k4=k4, pv=pv: e.copy(out=mixT[:, k4 * 4:(k4 + 1) * 4, t * 128:(t + 1) * 128], in_=pv),
                             r=[PB(bk)], w=[("mixT", t)])
                    else:
                        P.op("dve", lambda e, t=t, k4=k4, pv=pv: e.tensor_copy(out=mixT[:, k4 * 4:(k4 + 1) * 4, t * 128:(t + 1) * 128], in_=pv),
                             r=[PB(bk)], w=[("mixT", t)])
            dcn = 0
            for cb in range(4):
                ws = cb % 2
                if cb + 1 < 4:
                    load_wblock(wD, cb + 1, wo3, (cb + 1) * 512, 512, "wD")
                wr = wres("wD", ws)
                for t in range(NT):
                    bk = dcn % 4
                    xs = dcn % 3
                    dcn += 1
                    P.op("sp", lambda e, xs=xs, t=t, cb=cb, xsrc=xsrc: e.dma_start(out=xr[xs][:], in_=xsrc[t * 128:(t + 1) * 128, cb * 512:(cb + 1) * 512]),
                         r=[("y", t, cb)], w=[("xr", xs)], dma="xr%d" % xs)
                    for k in range(KCH):
                        P.op("pe", lambda e, ws=ws, k=k, t=t, bk=bk: e.matmul(
                            pbank[bk][:], lhsT=mixT[:, k, t * 128:(t + 1) * 128], rhs=wD[ws][:, k, :],
                            start=(k == 0), stop=(k == KCH - 1)), r=wr + [("mixT", t)], w=[PB(bk)])
                    P.op("dve", lambda e, xs=xs, bk=bk: e.tensor_tensor(out=xr[xs][:], in0=xr[xs][:], in1=pbank[bk][:], op=ALU.add),
                         r=[PB(bk), ("xr", xs)], w=[("xr", xs)])
                    P.op("act", lambda e, xs=xs, t=t, cb=cb: e.dma_start(out=y[t * 128:(t + 1) * 128, cb * 512:(cb + 1) * 512], in_=xr[xs][:]),
                         r=[("xr", xs)], w=[("y", t, cb)], dma="xw%d" % xs)
            P.barrier()
            if stop == "D":
                break

            AR.reset(PH)
            h2T = AR.alloc("hT", [128, KCH, S], BF16)
            norm_transpose(y, norm_ffn, l, h2T, "n2")

            wu3 = w_up[l].rearrange("(k p) c -> p k c", p=128)
            wg = [AR.alloc("wg", [128, KCH, 512], BF16) for i in range(2)]
            wv = [AR.alloc("wv", [128, KCH, 512], BF16) for i in range(2)]
            fw = AR.alloc("fw", [128, 3, 88], F32)
            fb = AR.alloc("fb", [128, 88], F32)
            ug = [AR.alloc("ug", [128, 2 + S], F32) for i in range(2)]
            uv = [AR.alloc("uv", [128, 2 + S], F32) for i in range(2)]
            cg = AR.alloc("cg", [128, S], F32)
            cv = AR.alloc("cv", [128, S], F32)
            sgl = AR.alloc("sgl", [128, S], F32)
            ast = [AR.alloc("ast", [128, S], BF16) for i in range(2)]
            for tap in range(3):
                P.op("sp", lambda e, tap=tap, l=l: e.dma_start(out=fw[:, tap, :], in_=conv_ffn_w[l][tap].rearrange("(c p) -> p c", p=128),
                                                           allow_slow_non_contiguous=True), w=["fw"], dma="fw")
            P.op("sp", lambda e, l=l: e.dma_start(out=fb[:], in_=conv_ffn_b[l].rearrange("(c p) -> p c", p=128),
                                              allow_slow_non_contiguous=True), w=["fb"], dma="fb")
            for i in range(2):
                P.op("pool", lambda e, i=i: e.memset(ug[i][:, 0:2], 0.0), w=[("ug", i)])
                P.op("pool", lambda e, i=i: e.memset(uv[i][:, 0:2], 0.0), w=[("uv", i)])
            NBLK = DFF // 512
            load_wblock(wg, 0, wu3, 0, 512, "wg")
            load_wblock(wv, 0, wu3, DFF, 512, "wv")
            fcn = 0
            pcn = 0
            for bb in range(NBLK):
                ws = bb % 2
                if bb + 1 < NBLK:
                    load_wblock(wg, bb + 1, wu3, (bb + 1) * 512, 512, "wg")
                    load_wblock(wv, bb + 1, wu3, DFF + (bb + 1) * 512, 512, "wv")
                for j in range(4):
                    chg = bb * 4 + j
                    us = pcn % 2
                    pcn += 1
                    for (wt, wtag, ub, utag) in ((wg, "wg", ug, "ug"), (wv, "wv", uv, "uv")):
                        wr = wres(wtag, ws)
                        for n in range(NG):
                            bk = fcn % 8
                            fcn += 1
                            for k in range(KCH):
                                P.op("pe", lambda e, wt=wt, ws=ws, k=k, j=j, n=n, bk=bk: e.matmul(
                                    pbank[bk][:], lhsT=wt[ws][:, k, j * 128:(j + 1) * 128],
                                    rhs=h2T[:, k, n * 512:(n + 1) * 512], start=(k == 0), stop=(k == KCH - 1)),
                                    r=wr + ["hT"], w=[PB(bk)])
                            P.op("act", lambda e, ub=ub, us=us, n=n, bk=bk: e.copy(out=ub[us][:, 2 + n * 512:2 + (n + 1) * 512], in_=pbank[bk][:]),
                                 r=[PB(bk)], w=[(utag, us)])
                    P.op("dve", lambda e, us=us, chg=chg: e.tensor_scalar(
                        out=cg[:], in0=ug[us][:, 2:2 + S], scalar1=fw[:, 2, chg:chg + 1], scalar2=fb[:, chg:chg + 1],
                        op0=ALU.mult, op1=ALU.add), r=[("ug", us), "fw", "fb"], w=["cg"])
                    for tap in (1, 0):
                        P.op("dve", lambda e, us=us, chg=chg, tap=tap: e.scalar_tensor_tensor(
                            out=cg[:], in0=ug[us][:, tap:tap + S], scalar=fw[:, tap, chg:chg + 1], in1=cg[:],
                            op0=ALU.mult, op1=ALU.add), r=[("ug", us), "cg", "fw"], w=["cg"])
                    chv = 44 + chg
                    P.op("pool", lambda e, us=us, chv=chv: e.tensor_scalar(
                        out=cv[:], in0=uv[us][:, 2:2 + S], scalar1=fw[:, 2, chv:chv + 1], scalar2=fb[:, chv:chv + 1],
                        op0=ALU.mult, op1=ALU.add), r=[("uv", us), "fw", "fb"], w=["cv"])
                    for tap in (1, 0):
                        P.op("dve", lambda e, us=us, chv=chv, tap=tap: e.scalar_tensor_tensor(
                            out=cv[:], in0=uv[us][:, tap:tap + S], scalar=fw[:, tap, chv:chv + 1], in1=cv[:],
                            op0=ALU.mult, op1=ALU.add), r=[("uv", us), "cv", "fw"], w=["cv"])
                    P.op("act", lambda e: e.activation(out=sgl[:], in_=cg[:], func=AF.Silu), r=["cg"], w=["sgl"])
                    P.op("pool", lambda e, us=us: e.tensor_tensor(out=ast[us][:], in0=sgl[:], in1=cv[:], op=ALU.mult),
                         r=["sgl", "cv"], w=[("ast", us)])
                    P.op("sp", lambda e, us=us, chg=chg: e.dma_start(out=a_fm[chg * 128:(chg + 1) * 128, :], in_=ast[us][:]),
                         r=[("ast", us)], w=[("a_fm", chg)], dma="ast%d" % us)
            P.barrier()
            if stop == "F1":
                break

            AR.reset(PH)
            KF = DFF // 128
            wd = [AR.alloc("wd", [128, KF, 512], BF16) for i in range(2)]
            at = [AR.alloc("at", [128, KF, 512], BF16) for i in range(2)]
            xr2 = [AR.alloc("xr2", [128, 512], F32) for i in range(3)]
            wd3 = w_down[l].rearrange("(k p) c -> p k c", p=128)
            a3 = a_fm.rearrange("(k p) t -> p k t", p=128)
            load_wblock(wd, 0, wd3, 0, 512, "wd", kn=KF, nsplit=4)
            dcn = 0
            acn = 0
            for cb in range(4):
                ws = cb % 2
                if cb + 1 < 4:
                    load_wblock(wd, cb + 1, wd3, (cb + 1) * 512, 512, "wd", kn=KF, nsplit=4)
                wr = wres("wd", ws, 4)
                for tp in range(NT // 4):
                    as_ = acn % 2
                    acn += 1
                    for hh in range(2):
                        P.op("sp", lambda e, as_=as_, tp=tp, hh=hh: e.dma_start(
                            out=at[as_][:, hh * 22:(hh + 1) * 22, :], in_=a3[:, hh * 22:(hh + 1) * 22, tp * 512:(tp + 1) * 512]),
                            w=[("at", as_, hh)], dma="at%d_%d" % (as_, hh))
                    for tt in range(4):
                        t = tp * 4 + tt
                        bk = dcn % 4
                        xs = dcn % 3
                        dcn += 1
                        P.op("act", lambda e, xs=xs, t=t, cb=cb: e.dma_start(out=xr2[xs][:], in_=y[t * 128:(t + 1) * 128, cb * 512:(cb + 1) * 512]),
                             r=[("y", t, cb)], w=[("xr2", xs)], dma="xr2%d" % xs)
                        for k in range(KF):
                            P.op("pe", lambda e, ws=ws, k=k, tt=tt, as_=as_, bk=bk: e.matmul(
                                pbank[bk][:], lhsT=at[as_][:, k, tt * 128:(tt + 1) * 128], rhs=wd[ws][:, k, :],
                                start=(k == 0), stop=(k == KF - 1)), r=wr + [("at", as_, 0), ("at", as_, 1)], w=[PB(bk)])
                        P.op("dve", lambda e, xs=xs, bk=bk: e.tensor_tensor(out=xr2[xs][:], in0=xr2[xs][:], in1=pbank[bk][:], op=ALU.add),
                             r=[PB(bk), ("xr2", xs)], w=[("xr2", xs)])
                        P.op("act", lambda e, xs=xs, t=t, cb=cb: e.dma_start(out=y[t * 128:(t + 1) * 128, cb * 512:(cb + 1) * 512], in_=xr2[xs][:]),
                             r=[("xr2", xs)], w=[("y", t, cb)], dma="xw2%d" % xs)
            P.barrier()
        if "gates" in dbg_outs:
            P.op("sp", lambda e: e.dma_start(out=dbg_outs["gates"], in_=gates[:]), r=["gates"], dma="dbgg")
        P.barrier()
        P.emit()
    return nc


_WNAMES = ["norm_mix", "w_in", "gate_bias", "conv_qk_w", "conv_qk_b", "rel_bias", "w_out", "norm_ffn",
           "w_up", "conv_ffn_w", "conv_ffn_b", "w_down"]


PLACE = [0, 1, 4, 5]


def kernel(**inputs):
    x = np.asarray(inputs["x"], dtype=np.float32)
    B, S, _ = x.shape
    L = int(np.asarray(inputs["w_in"]).shape[0])
    nc = build_program(S=S, L=L)
    consts = host_consts(S)
    shared = {k: np.ascontiguousarray(np.asarray(inputs[k], dtype=np.float32)) for k in _WNAMES}
    shared["mlstm_norm"] = np.ascontiguousarray(np.asarray(inputs["mlstm_norm"], dtype=np.float32).reshape(L, 1024))
    shared["qk_norm"] = np.ascontiguousarray(np.asarray(inputs["qk_norm"], dtype=np.float32).reshape(L, 256))
    for k, v in consts.items():
        shared["c_" + k] = v
    ncores = 8 if B == 4 else B
    place = PLACE if B == 4 else list(range(B))
    zero_x = np.zeros((S, x.shape[2]), np.float32)
    pad = dict(shared)
    for k in ("w_in", "w_out", "w_up", "w_down", "conv_qk_b", "conv_ffn_b"):
        pad[k] = np.zeros_like(shared[k])
    pad["x"] = zero_x
    in_maps = []
    for c in range(ncores):
        if c in place:
            m = dict(shared)
            m["x"] = np.ascontiguousarray(x[place.index(c)])
        else:
            m = pad
        in_maps.append(m)
    res = run_bass_kernel_spmd(nc, in_maps, core_ids=list(range(ncores)))
    return np.stack([np.asarray(res.results[place[b]]["y"], dtype=np.float32) for b in range(B)], axis=0)
```

```python
import numpy as np
from contextlib import ExitStack
import concourse.bass as bass
import concourse.mybir as mybir
from concourse.bass_utils import run_bass_kernel_spmd

F32 = mybir.dt.float32
BF16 = mybir.dt.bfloat16
AF = mybir.ActivationFunctionType
ALU = mybir.AluOpType
AX = mybir.AxisListType

ENGS = ("pe", "act", "dve", "pool", "sp")


class Op:
    __slots__ = ("eng", "fn", "deps", "dma", "sem", "val", "signaled", "waits", "n")

    def __init__(self, eng, fn, dma):
        self.eng = eng
        self.fn = fn
        self.dma = dma
        self.deps = []
        self.sem = None
        self.val = 0
        self.signaled = False
        self.waits = []


class Prog:
    def __init__(self, nc, stack):
        self.nc = nc
        self.stack = stack
        self.ops = {e: [] for e in ENGS}
        self.lastw = {}
        self.readers = {}
        self.dma_keys = {}
        self.nops = 0

    def sem(self, name):
        return self.stack.enter_context(self.nc.semaphore(name))

    def op(self, eng, fn, r=(), w=(), dma=None):
        o = Op(eng, fn, dma)
        if dma is not None:
            o.signaled = True
        o.n = self.nops
        self.nops += 1
        deps = {}
        for k in r:
            lw = self.lastw.get(k)
            if lw is not None:
                deps[id(lw)] = lw
        for k in w:
            lw = self.lastw.get(k)
            if lw is not None:
                deps[id(lw)] = lw
            for rd in self.readers.get(k, {}).values():
                deps[id(rd)] = rd
        for d in deps.values():
            if d is o:
                continue
            if d.eng == "pe" and eng == "pe" and d.dma is None and dma is None:
                continue
            d.signaled = True
            o.deps.append(d)
        rk = eng if dma is None else ("dma", dma)
        for k in r:
            self.readers.setdefault(k, {})[rk] = o
        for k in w:
            self.lastw[k] = o
            self.readers[k] = {}
        self.ops[eng].append(o)
        return o

    def barrier(self):
        lasts = []
        for e in ENGS:
            seen_c = False
            seen_d = set()
            for o in reversed(self.ops[e]):
                if o.fn is None:
                    break
                if o.dma is None:
                    if not seen_c:
                        seen_c = True
                        lasts.append(o)
                else:
                    if o.dma not in seen_d:
                        seen_d.add(o.dma)
                        lasts.append(o)
        for e in ENGS:
            b = Op(e, None, None)
            b.n = self.nops
            self.nops += 1
            for d in lasts:
                d.signaled = True
                b.deps.append(d)
            self.ops[e].append(b)
        self.lastw = {}
        self.readers = {}

    def emit(self):
        nc = self.nc
        eng_sem = {e: self.sem("c_" + e) for e in ENGS}
        cnt = {e: 0 for e in ENGS}
        dsem = {}
        dcnt = {}
        allops = sorted((o for e in ENGS for o in self.ops[e]), key=lambda o: o.n)
        for o in allops:
            if not o.signaled:
                continue
            if o.dma is not None:
                if o.dma not in dsem:
                    dsem[o.dma] = self.sem("d_%s" % (o.dma,))
                    dcnt[o.dma] = 0
                dcnt[o.dma] += 16
                o.sem = dsem[o.dma]
                o.val = dcnt[o.dma]
            else:
                cnt[o.eng] += 1
                o.sem = eng_sem[o.eng]
                o.val = cnt[o.eng]
        handles = {"pe": nc.tensor, "act": nc.scalar, "dve": nc.vector,
                   "pool": nc.gpsimd, "sp": nc.sync}
        self.maxval = dict(cnt)
        self.ndsem = len(dsem)
        with nc.Block() as block:
            def make(e):
                def body(eng):
                    waited = {}
                    for o in self.ops[e]:
                        for d in o.deps:
                            key = id(d.sem)
                            if waited.get(key, 0) >= d.val:
                                continue
                            waited[key] = d.val
                            eng.wait_ge(d.sem, d.val)
                        if o.fn is None:
                            continue
                        ins = o.fn(eng)
                        if o.signaled:
                            ins.then_inc(o.sem, 16 if o.dma is not None else 1)
                return body
            block.tensor(make("pe"))
            block.scalar(make("act"))
            block.vector(make("dve"))
            block.gpsimd(make("pool"))
            block.sync(make("sp"))


D = 2048
DIN = 7176
DFF = 5632
NBUCK = 32
EPS = 1e-6
KCH = D // 128
NEG = -30000.0
FOFF = 638


def t5_bucket_np(d):
    d = np.maximum(d, 0)
    ratio = np.maximum(d, 16).astype(np.float32) / np.float32(16)
    large = 16 + (np.log(ratio) / np.float32(np.log(128.0)) * 16).astype(np.int32)
    return np.where(d < 16, d, np.minimum(large, 31))


def host_consts(S):
    NT = S // 128
    NB = S // 256
    W2 = S + FOFF + 2
    c = {}
    c["ident_f"] = np.eye(128, dtype=np.float32)
    s = np.arange(128)
    c["tri_f"] = (s[:, None] <= s[None, :]).astype(np.float32)
    c["ones_f"] = np.ones((128, 128), np.float32)
    oh = np.zeros((33, W2), np.float32)
    i = np.arange(W2)
    dist = i - FOFF
    b = t5_bucket_np(dist)
    for k in range(W2):
        if dist[k] >= 0:
            oh[b[k], k] = 1.0
        else:
            oh[32, k] = 1.0
    c["oh"] = oh
    qb = (np.arange(NT) // 2)
    j = np.arange(NB)
    past = (j[None, :] < qb[:, None])
    c["pastadd"] = np.where(past, 0.0, -1e30).astype(np.float32).reshape(1, NT * NB)
    c["pastneg"] = np.where(past, NEG, 0.0).astype(np.float32).reshape(1, NT * NB)
    e = np.zeros((8, NB, 128), np.float32)
    for jj in range(NB):
        e[jj, jj, :] = 1.0
    c["eall"] = e.reshape(8, NB * 128)
    return c


class Arena:
    def __init__(self, nc, lo, hi):
        self.nc, self.lo, self.hi, self.cur, self.n = nc, lo, hi, lo, 0

    def alloc(self, name, shape, dt):
        nb = 4 if dt == F32 else 2
        for d in shape[1:]:
            nb *= d
        off = (self.cur + 63) // 64 * 64
        assert off + nb <= self.hi, "SBUF arena overflow: %s needs %d at %d (hi %d)" % (name, nb, off, self.hi)
        self.cur = off + nb
        self.n += 1
        return self.nc.alloc_sbuf_tensor_at("%s_%d" % (name, self.n), list(shape), dt, offset=off)

    def mark(self):
        return self.cur

    def reset(self, m):
        self.cur = m


def build_program(S=2048, L=4, dbg=None):
    dbg = dbg or {}
    NT = S // 128
    NG = S // 512
    NB = S // 256
    W2 = S + FOFF + 2
    MW = S + 511
    nc = bass.Bass("TRN2", target_bir_lowering=False)

    def din(name, shape):
        return nc.dram_tensor(name, list(shape), F32, kind="ExternalInput").ap()

    x_in = din("x", [S, D])
    norm_mix = din("norm_mix", [L, D])
    w_in = din("w_in", [L, D, DIN])
    gate_bias = din("gate_bias", [L, 8])
    conv_qk_w = din("conv_qk_w", [L, 4, 2048])
    conv_qk_b = din("conv_qk_b", [L, 2048])
    mlstm_norm = din("mlstm_norm", [L, 1024])
    qk_norm = din("qk_norm", [L, 256])
    rel_bias = din("rel_bias", [32, 8])
    w_out = din("w_out", [L, D, D])
    norm_ffn = din("norm_ffn", [L, D])
    w_up = din("w_up", [L, D, 2 * DFF])
    conv_ffn_w = din("conv_ffn_w", [L, 3, 2 * DFF])
    conv_ffn_b = din("conv_ffn_b", [L, 2 * DFF])
    w_down = din("w_down", [L, DFF, D])
    c_ident = din("c_ident_f", [128, 128])
    c_tri = din("c_tri_f", [128, 128])
    c_ones = din("c_ones_f", [128, 128])
    c_oh = din("c_oh", [33, W2])
    c_pastadd = din("c_pastadd", [1, NT * NB])
    c_pastneg = din("c_pastneg", [1, NT * NB])
    c_eall = din("c_eall", [8, NB * 128])
    y = nc.dram_tensor("y", [S, D], F32, kind="ExternalOutput").ap()

    skind = "ExternalOutput" if dbg.get("scratch") else "Internal"
    qk_fm = nc.dram_tensor("qk_fm", [2048, S], BF16, kind=skind).ap()
    v_tm = nc.dram_tensor("v_tm", [S, 1024], BF16, kind=skind).ap()
    mo_tm = nc.dram_tensor("mo_tm", [S, 1024], BF16, kind=skind).ap()
    aqk_fm = nc.dram_tensor("aqk_fm", [2048, S], BF16, kind=skind).ap()
    av_tm = nc.dram_tensor("av_tm", [S, 1024], BF16, kind=skind).ap()
    a_fm = nc.dram_tensor("a_fm", [DFF, S], BF16, kind=skind).ap()
    d2 = nc.dram_tensor("d2", [8, 128, W2], F32, kind=skind).ap()
    dbg_outs = {}
    for name, shape in dbg.items():
        if isinstance(shape, (list, tuple)):
            dbg_outs[name] = nc.dram_tensor("dbg_" + name, list(shape), F32, kind="ExternalOutput").ap()

    def bcast_rows(ap2d, row, n):
        return bass.AP(ap2d.tensor, row * ap2d.shape[1], [[0, 128], [1, n]])

    stop = dbg.get("stop")
    with ExitStack() as top:
        P = Prog(nc, top)
        AR = Arena(nc, 16512, 229344)
        pbank = [nc.alloc_psum_tensor("pb%d" % b, [128, 512], F32) for b in range(8)]

        def PB(b):
            return ("pb", b)

        def pb16(b):
            return pbank[b][:].bitcast(BF16)

        ident_f = AR.alloc("ident_f", [128, 128], F32)
        ident_b = AR.alloc("ident_b", [128, 128], BF16)
        tri_f = AR.alloc("tri_f", [128, 128], F32)
        ones_f = AR.alloc("ones_f", [128, 128], F32)
        eall = AR.alloc("eall", [8, NB * 128], BF16)
        pastadd = AR.alloc("pastadd", [128, NT * NB], F32)
        pastneg = AR.alloc("pastneg", [128, NT * NB], F32)
        gates = AR.alloc("gates", [128, NT, 8], F32)
        epsc = AR.alloc("epsc", [128, 4], F32)
        P.op("pool", lambda e: e.memset(epsc[:, 0:1], D * EPS), w=["epsc"])
        P.op("pool", lambda e: e.memset(epsc[:, 1:2], 128.0 * EPS), w=["epsc"])
        P.op("pool", lambda e: e.memset(epsc[:, 2:3], 256.0 * EPS), w=["epsc"])
        P.op("pool", lambda e: e.memset(epsc[:, 3:4], 1.0), w=["epsc"])
        P.op("sp", lambda e: e.dma_start(out=ident_f[:], in_=c_ident), w=["ident_f"], dma="c0")
        P.op("sp", lambda e: e.dma_start(out=tri_f[:], in_=c_tri), w=["tri_f"], dma="c1")
        P.op("sp", lambda e: e.dma_start(out=ones_f[:], in_=c_ones), w=["ones_f"], dma="c2")
        P.op("pool", lambda e: e.dma_start(out=eall[:], in_=c_eall), w=["eall"], dma="c3")
        P.op("sp", lambda e: e.dma_start(out=pastadd[:], in_=bcast_rows(c_pastadd, 0, NT * NB)), w=["pastadd"], dma="c4")
        P.op("sp", lambda e: e.dma_start(out=pastneg[:], in_=bcast_rows(c_pastneg, 0, NT * NB)), w=["pastneg"], dma="c5")
        P.op("dve", lambda e: e.tensor_copy(out=ident_b[:], in_=ident_f[:]), r=["ident_f"], w=["ident_b"])
        PH = AR.mark()

        oh_sb = AR.alloc("oh_sb", [33, W2], F32)
        rb = AR.alloc("rb", [33, 8], F32)
        rbrep = AR.alloc("rbrep", [33, 8, 128], F32)
        frep = [AR.alloc("frep", [128, W2], F32) for i in range(2)]
        P.op("sp", lambda e: e.dma_start(out=oh_sb[:], in_=c_oh), w=["oh_sb"], dma="t0")
        P.op("pool", lambda e: e.memset(rb[:], -200.0), w=["rb"])
        P.op("sp", lambda e: e.dma_start(out=rb[0:32, :], in_=rel_bias), w=["rb"], dma="t1")
        for h in range(8):
            P.op("dve", lambda e, h=h: e.tensor_copy(out=rbrep[:, h, :], in_=rb[:, h:h + 1].to_broadcast([33, 128])),
                 r=["rb"], w=[("rbrep", h)])
        nck = (W2 + 511) // 512
        for h in range(8):
            fr = frep[h % 2]
            for ci in range(nck):
                c0 = ci * 512
                cw = min(512, W2 - c0)
                bk = ci % 2
                P.op("pe", lambda e, h=h, c0=c0, cw=cw, bk=bk: e.matmul(
                    pbank[bk][:, 0:cw], lhsT=rbrep[:, h, :], rhs=oh_sb[:, c0:c0 + cw], start=True, stop=True),
                    r=[("rbrep", h), "oh_sb"], w=[PB(bk)])
                P.op("act", lambda e, fr=fr, c0=c0, cw=cw, bk=bk: e.activation(
                    out=fr[:, c0:c0 + cw], in_=pbank[bk][:, 0:cw], func=AF.Exp),
                    r=[PB(bk)], w=[("frep", h % 2)])
            P.op("sp", lambda e, h=h, fr=fr: e.dma_start(out=d2[h], in_=fr[:]),
                 r=[("frep", h % 2)], w=[("d2", h)], dma="fst%d" % (h % 2))
        P.barrier()
        AR.reset(PH)

        def norm_transpose(src_ap, gvec_ap, lrow, hT, tag):
            m0 = AR.mark()
            gb = AR.alloc(tag + "gb", [128, D], F32)
            NXB = 4
            xt = [AR.alloc(tag + "xt", [128, D], F32) for i in range(NXB)]
            junk = AR.alloc(tag + "junk", [128, D], BF16)
            hb = [AR.alloc(tag + "hb", [128, D], BF16) for i in range(NXB)]
            ss = [AR.alloc(tag + "ss", [128, 4], F32) for i in range(NXB)]
            P.op("sp", lambda e: e.dma_start(out=gb[:], in_=bcast_rows(gvec_ap, lrow, D)), w=[tag + "gb"], dma=tag + "gb")
            P.op("dve", lambda e: e.tensor_scalar(out=gb[:], in0=gb[:], scalar1=float(D) ** 0.5, scalar2=None, op0=ALU.mult),
                 r=[tag + "gb"], w=[tag + "gb"])
            def n_stage1(t):
                s = t % NXB
                P.op("sp" if t % 2 == 0 else "act", lambda e: e.dma_start(out=xt[s][:], in_=src_ap[t * 128:(t + 1) * 128, :]),
                     w=[(tag + "xt", s)], dma=tag + "xt%d" % s)
                P.op("act", lambda e: e.activation(out=junk[:], in_=xt[s][:], func=AF.Square, accum_out=ss[s][:, 0:1]),
                     r=[(tag + "xt", s)], w=[tag + "junk", (tag + "ss", s)])
                P.op("act", lambda e: e.activation(out=ss[s][:, 1:2], in_=ss[s][:, 0:1], func=AF.Sqrt, bias=epsc[:, 0:1]),
                     r=[(tag + "ss", s), "epsc"], w=[(tag + "ss1", s)])
                P.op("dve", lambda e: e.reciprocal(out=ss[s][:, 2:3], in_=ss[s][:, 1:2]),
                     r=[(tag + "ss1", s)], w=[(tag + "ss2", s)])
                P.op("dve", lambda e: e.scalar_tensor_tensor(out=hb[s][:], in0=xt[s][:], scalar=ss[s][:, 2:3], in1=gb[:],
                                                             op0=ALU.mult, op1=ALU.mult),
                     r=[(tag + "xt", s), (tag + "ss2", s), tag + "gb"], w=[(tag + "hb", s)])

            def n_stage2(t):
                s = t % NXB
                for k4 in range(KCH // 4):
                    bk = 4 + k4 % 4
                    pv = pb16(bk)[:, 0:512].rearrange("p (a b) -> p a b", b=128)
                    for kk in range(4):
                        k = k4 * 4 + kk
                        P.op("pe", lambda e, k=k, kk=kk, pv=pv: e.transpose(
                            pv[:, kk, :], hb[s][:, k * 128:(k + 1) * 128], ident_b[:]),
                            r=[(tag + "hb", s), "ident_b"], w=[PB(bk)])
                    if k4 % 2 == 0:
                        P.op("act", lambda e, k4=k4, pv=pv: e.copy(
                            out=hT[:, k4 * 4:(k4 + 1) * 4, t * 128:(t + 1) * 128], in_=pv),
                            r=[PB(bk)], w=[("hT", t)])
                    else:
                        P.op("dve", lambda e, k4=k4, pv=pv: e.tensor_copy(
                            out=hT[:, k4 * 4:(k4 + 1) * 4, t * 128:(t + 1) * 128], in_=pv),
                            r=[PB(bk)], w=[("hT", t)])

            NLOOK = 2
            for t in range(NT + NLOOK):
                if t < NT:
                    n_stage1(t)
                if t - NLOOK >= 0:
                    n_stage2(t - NLOOK)
            P.barrier()
            AR.reset(m0)

        def load_wblock(wslots, idx, src3, c0, cw, tag, kn=KCH, nsplit=2):
            s = idx % len(wslots)
            step = (kn + nsplit - 1) // nsplit
            for hh in range(nsplit):
                k0, k1 = hh * step, min(kn, (hh + 1) * step)
                P.op("pool", lambda e, s=s, k0=k0, k1=k1: e.dma_start(
                    out=wslots[s][:, k0:k1, 0:cw], in_=src3[:, k0:k1, c0:c0 + cw]),
                    w=[(tag, s, hh)], dma="%s%d_%d" % (tag, s, hh))
            return s

        def wres(tag, s, nsplit=2):
            return [(tag, s, hh) for hh in range(nsplit)]

        for l in range(L if stop != "T" else 0):
            xsrc = x_in if l == 0 else y
            AR.reset(PH)
            hT = AR.alloc("hT", [128, KCH, S], BF16)
            win3 = w_in[l].rearrange("(k p) c -> p k c", p=128)
            wsl = [AR.alloc("wA", [128, KCH, 512], BF16) for i in range(2)]
            load_wblock(wsl, 0, win3, 0, 512, "wA")
            norm_transpose(xsrc, norm_mix, l, hT, "n1")
            if stop == "A1":
                break
            cw_sb = AR.alloc("cw_sb", [128, 4, 16], F32)
            cb_sb = AR.alloc("cb_sb", [128, 16], F32)
            gq = AR.alloc("gq", [128, 2], F32)
            gbias = AR.alloc("gbias", [128, 8], F32)
            u = [AR.alloc("u", [128, 3 + S], F32) for i in range(2)]
            acc = [AR.alloc("acc", [128, S], F32) for i in range(2)]
            sg = AR.alloc("sg", [128, S], F32)
            ofm = [AR.alloc("ofm", [128, S], BF16) for i in range(2)]
            otm = [AR.alloc("otm", [128, 512], BF16) for i in range(3)]
            zq = [AR.alloc("zq", [128, 512], F32) for i in range(2)]
            sq = [AR.alloc("sq", [128, 512], F32) for i in range(2)]
            rs = [AR.alloc("rs", [128, 512], F32) for i in range(2)]
            ps_g = pbank[7][:, 0:NT * 8].rearrange("p (t g) -> p t g", g=8)
            for tap in range(4):
                P.op("sp", lambda e, tap=tap, l=l: e.dma_start(out=cw_sb[:, tap, :], in_=conv_qk_w[l][tap].rearrange("(c p) -> p c", p=128),
                                                           allow_slow_non_contiguous=True), w=["cw_sb"], dma="cw")
            P.op("sp", lambda e, l=l: e.dma_start(out=cb_sb[:], in_=conv_qk_b[l].rearrange("(c p) -> p c", p=128),
                                              allow_slow_non_contiguous=True), w=["cb_sb"], dma="cb")
            P.op("dve", lambda e: e.tensor_scalar(out=cw_sb[:, :, 0:8], in0=cw_sb[:, :, 0:8], scalar1=1.0 / 16.0, scalar2=None,
                                                  op0=ALU.mult), r=["cw_sb"], w=["cw_sb"])
            P.op("dve", lambda e: e.tensor_scalar(out=cb_sb[:, 0:8], in0=cb_sb[:, 0:8], scalar1=1.0 / 16.0, scalar2=None,
                                                  op0=ALU.mult), r=["cb_sb"], w=["cb_sb"])
            P.op("sp", lambda e, l=l: e.dma_start(out=gq[:], in_=qk_norm[l].rearrange("(c p) -> p c", p=128),
                                              allow_slow_non_contiguous=True), w=["gq"], dma="gq")
            P.op("dve", lambda e: e.tensor_scalar(out=gq[:, 1:2], in0=gq[:, 1:2], scalar1=128.0 ** 0.5, scalar2=None,
                                                  op0=ALU.mult), r=["gq"], w=["gq"])
            P.op("sp", lambda e, l=l: e.dma_start(out=gbias[:], in_=bcast_rows(gate_bias, l, 8)), w=["gbias"], dma="gbias")
            for i in range(2):
                P.op("pool", lambda e, i=i: e.memset(u[i][:, 0:3], 0.0), w=[("u", i)])
            mcnt = [0]

            def mbank():
                b = mcnt[0] % 5
                mcnt[0] += 1
                return b

            blocks = [("q", i * 512, 512) for i in range(2)] + [("k", 1024 + i * 512, 512) for i in range(2)] + \
                     [("v", 2048 + i * 512, 512) for i in range(2)] + [("o", 3072 + i * 512, 512) for i in range(2)] + \
                     [("g", 4096, 8)] + [("aq", 4104 + i * 512, 512) for i in range(2)] + \
                     [("ak", 5128 + i * 512, 512) for i in range(2)] + [("av", 6152 + i * 512, 512) for i in range(2)]
            fmc = 0
            tmc = 0
            nq = 0
            pend_tail = []
            for bi, (kind, c0, cw) in enumerate(blocks):
                ws = bi % 2
                if kind == "av" and pend_tail:
                    pend_tail.pop()()
                if bi + 1 < len(blocks):
                    load_wblock(wsl, bi + 1, win3, blocks[bi + 1][1], blocks[bi + 1][2], "wA")
                wr = wres("wA", ws)
                if kind in ("q", "k"):
                    for j in range(4):
                        ch = (c0 + j * 128) // 128
                        us = fmc % 2
                        fmc += 1
                        for n in range(NG):
                            bk = mbank()
                            for k in range(KCH):
                                P.op("pe", lambda e, ws=ws, k=k, j=j, n=n, bk=bk: e.matmul(
                                    pbank[bk][:], lhsT=wsl[ws][:, k, j * 128:(j + 1) * 128],
                                    rhs=hT[:, k, n * 512:(n + 1) * 512], start=(k == 0), stop=(k == KCH - 1)),
                                    r=wr + ["hT"], w=[PB(bk)])
                            P.op("act", lambda e, us=us, n=n, bk=bk: e.copy(out=u[us][:, 3 + n * 512:3 + (n + 1) * 512],
                                                                             in_=pbank[bk][:]),
                                 r=[PB(bk)], w=[("u", us)])
                        ac = acc[us]
                        P.op("dve", lambda e, us=us, ch=ch, ac=ac: e.tensor_scalar(
                            out=ac[:], in0=u[us][:, 3:3 + S], scalar1=cw_sb[:, 3, ch:ch + 1], scalar2=cb_sb[:, ch:ch + 1],
                            op0=ALU.mult, op1=ALU.add), r=[("u", us), "cw_sb", "cb_sb"], w=[("acc", us)])
                        for tap in (2, 1, 0):
                            P.op("dve", lambda e, us=us, ch=ch, ac=ac, tap=tap: e.scalar_tensor_tensor(
                                out=ac[:], in0=u[us][:, tap:tap + S], scalar=cw_sb[:, tap, ch:ch + 1], in1=ac[:],
                                op0=ALU.mult, op1=ALU.add), r=[("u", us), ("acc", us), "cw_sb"], w=[("acc", us)])
                        if kind == "q":
                            P.op("act", lambda e, ac=ac: e.activation(out=sg[:], in_=ac[:], func=AF.Sigmoid, scale=16.0),
                                 r=[("acc", us)], w=["sg"])
                            P.op("pool", lambda e, ac=ac, us=us: e.tensor_tensor(out=ofm[us][:], in0=ac[:], in1=sg[:], op=ALU.mult),
                                 r=[("acc", us), "sg"], w=[("ofm", us)])
                        else:
                            P.op("act", lambda e, ac=ac, us=us: e.activation(out=ofm[us][:], in_=ac[:], func=AF.Silu),
                                 r=[("acc", us)], w=[("ofm", us)])
                        P.op("sp", lambda e, us=us, ch=ch: e.dma_start(out=qk_fm[ch * 128:(ch + 1) * 128, :], in_=ofm[us][:]),
                             r=[("ofm", us)], w=[("qk_fm", ch)], dma="ofm%d" % us)
                elif kind in ("v", "o", "av"):
                    dst = {"v": v_tm, "o": mo_tm, "av": av_tm}[kind]
                    cc0 = c0 - {"v": 2048, "o": 3072, "av": 6152}[kind]
                    for t in range(NT):
                        bk = mbank()
                        for k in range(KCH):
                            P.op("pe", lambda e, ws=ws, k=k, t=t, bk=bk: e.matmul(
                                pbank[bk][:], lhsT=hT[:, k, t * 128:(t + 1) * 128], rhs=wsl[ws][:, k, :],
                                start=(k == 0), stop=(k == KCH - 1)), r=wr + ["hT"], w=[PB(bk)])
                        os_ = tmc % 3
                        tmc += 1
                        if kind == "o":
                            P.op("act", lambda e, os_=os_, bk=bk: e.activation(out=otm[os_][:], in_=pbank[bk][:], func=AF.Sigmoid),
                                 r=[PB(bk)], w=[("otm", os_)])
                        else:
                            P.op("act", lambda e, os_=os_, bk=bk: e.copy(out=otm[os_][:], in_=pbank[bk][:]),
                                 r=[PB(bk)], w=[("otm", os_)])
                        P.op("sp", lambda e, os_=os_, t=t, dst=dst, cc0=cc0: e.dma_start(
                            out=dst[t * 128:(t + 1) * 128, cc0:cc0 + 512], in_=otm[os_][:]),
                            r=[("otm", os_)], w=[(kind, t, cc0)], dma="otm%d" % os_)
                elif kind == "g":
                    for t in range(NT):
                        for k in range(KCH):
                            P.op("pe", lambda e, ws=ws, k=k, t=t: e.matmul(
                                ps_g[:, t, :], lhsT=hT[:, k, t * 128:(t + 1) * 128], rhs=wsl[ws][:, k, 0:8],
                                start=(k == 0), stop=(k == KCH - 1)), r=wr + ["hT"], w=[PB(7)])
                    P.op("dve", lambda e: e.tensor_tensor(out=gates[:], in0=ps_g, in1=gbias[:].unsqueeze(1).to_broadcast([128, NT, 8]),
                                                          op=ALU.add), r=[PB(7), "gbias"], w=["gates"])
                else:
                    isq = kind == "aq"
                    for j in range(4):
                        hd = (c0 - (4104 if isq else 5128)) // 128 + j
                        fs = fmc % 2
                        fmc += 1
                        for n in range(NG):
                            bk = mbank()
                            z = nq % 2
                            nq += 1
                            for k in range(KCH):
                                P.op("pe", lambda e, ws=ws, k=k, j=j, n=n, bk=bk: e.matmul(
                                    pbank[bk][:], lhsT=wsl[ws][:, k, j * 128:(j + 1) * 128],
                                    rhs=hT[:, k, n * 512:(n + 1) * 512], start=(k == 0), stop=(k == KCH - 1)),
                                    r=wr + ["hT"], w=[PB(bk)])
                            P.op("act", lambda e, z=z, bk=bk: e.copy(out=zq[z][:], in_=pbank[bk][:]),
                                 r=[PB(bk)], w=[("zq", z)])
                            P.op("act", lambda e, z=z, bk=bk: e.activation(out=sq[z][:], in_=pbank[bk][:], func=AF.Square),
                                 r=[PB(bk)], w=[("sq", z)])
                            gcol = gq[:, 0:1] if isq else gq[:, 1:2]
                            row = (0 if isq else 1024) + hd * 128

                            def tail(z=z, fs=fs, n=n, gcol=gcol, row=row):
                                P.op("pe", lambda e: e.matmul(pbank[5 + z][:], lhsT=ones_f[:], rhs=sq[z][:], start=True, stop=True),
                                     r=[("sq", z), "ones_f"], w=[PB(5 + z)])
                                P.op("act", lambda e: e.activation(out=rs[z][:], in_=pbank[5 + z][:], func=AF.Sqrt, bias=epsc[:, 1:2]),
                                     r=[PB(5 + z), "epsc"], w=[("rs", z)])
                                P.op("dve", lambda e: e.reciprocal(out=rs[z][:], in_=rs[z][:]),
                                     r=[("rs", z)], w=[("rs", z)])
                                P.op("dve", lambda e: e.scalar_tensor_tensor(
                                    out=ofm[fs][:, n * 512:(n + 1) * 512], in0=zq[z][:], scalar=gcol, in1=rs[z][:],
                                    op0=ALU.mult, op1=ALU.mult), r=[("zq", z), ("rs", z), "gq"], w=[("ofm", fs)])
                                if n == NG - 1:
                                    P.op("sp", lambda e: e.dma_start(out=aqk_fm[row:row + 128, :], in_=ofm[fs][:]),
                                         r=[("ofm", fs)], w=[("aqk_fm", row)], dma="ofm%d" % fs)
                            if pend_tail:
                                pend_tail.pop()()
                            pend_tail.append(tail)
            if pend_tail:
                pend_tail.pop()()
            P.barrier()
            if stop == "A":
                break

            AR.reset(PH)
            mix = AR.alloc("mix", [128, NT, 2048], BF16)
            MIXM = AR.mark()
            gnb = AR.alloc("gnb", [128, 1024], F32)
            lfn = AR.alloc("lfn", [128, NT, 4], F32)
            t1 = AR.alloc("t1", [128, NT, 4], F32)
            e_s = AR.alloc("e_s", [128, NT, 4], F32)
            ebi = AR.alloc("ebi", [128, NT, 4], F32)
            ebL = AR.alloc("ebL", [128, NT, 4], F32)
            qF = [AR.alloc("qF", [128, 2, S], BF16) for i in range(2)]
            kF = [AR.alloc("kF", [128, 2, S], BF16) for i in range(2)]
            vA = [AR.alloc("vA", [128, NT, 257], BF16) for i in range(2)]
            moS = [AR.alloc("moS", [128, NT, 256], BF16) for i in range(2)]
            SdT = [AR.alloc("SdT", [128, 128], BF16) for i in range(2)]
            kw = [AR.alloc("kw", [128, 256], BF16) for i in range(2)]
            Cf = AR.alloc("Cf", [128, 2, 257], F32)
            Cb = [AR.alloc("Cb", [128, 2, 257], BF16) for i in range(2)]
            sm = [AR.alloc("sm", [128, 8], F32) for i in range(2)]
            junk2 = AR.alloc("junk2", [128, 256], BF16)
            hn = [AR.alloc("hn", [128, 256], F32) for i in range(2)]
            P.op("sp", lambda e, l=l: e.dma_start(out=gnb[:], in_=bcast_rows(mlstm_norm, l, 1024)), w=["gnb"], dma="gnb")
            for i in range(2):
                P.op("pool", lambda e, i=i: e.memset(vA[i][:, :, 256:257], 1.0), w=[("vA", i)])
            P.op("act", lambda e: e.activation(out=t1[:], in_=gates[:, :, 4:8], func=AF.Exp, scale=-1.0), r=["gates"], w=["t1"])
            P.op("act", lambda e: e.activation(out=lfn[:], in_=t1[:], func=AF.Ln, bias=epsc[:, 3:4]), r=["t1", "epsc"], w=["lfn"])
            bn_ps = pbank[7][:, 0:NT * 4].rearrange("p (t g) -> p t g", g=4)
            bL_ps = pbank[6][:, 0:NT * 4].rearrange("p (t g) -> p t g", g=4)
            for t in range(NT):
                P.op("pe", lambda e, t=t: e.matmul(bn_ps[:, t, :], lhsT=tri_f[:], rhs=lfn[:, t, :], start=True, stop=True),
                     r=["lfn", "tri_f"], w=[PB(7)])
                P.op("pe", lambda e, t=t: e.matmul(bL_ps[:, t, :], lhsT=ones_f[:], rhs=lfn[:, t, :], start=True, stop=True),
                     r=["lfn", "ones_f"], w=[PB(6)])
            P.op("dve", lambda e: e.tensor_tensor(out=t1[:], in0=gates[:, :, 0:4], in1=bn_ps, op=ALU.add), r=["gates", PB(7), "t1"], w=["t1"])
            P.op("act", lambda e: e.activation(out=e_s[:], in_=t1[:], func=AF.Exp), r=["t1"], w=["e_s"])
            P.op("act", lambda e: e.activation(out=ebi[:], in_=bn_ps, func=AF.Exp), r=[PB(7)], w=["ebi"])
            P.op("act", lambda e: e.activation(out=ebL[:], in_=bL_ps, func=AF.Exp, scale=-1.0), r=[PB(6)], w=["ebL"])
            for hd in range(4):
                hs = hd % 2
                for dc in range(2):
                    P.op("sp", lambda e, hs=hs, dc=dc, hd=hd: e.dma_start(
                        out=qF[hs][:, dc, :], in_=qk_fm[hd * 256 + dc * 128:hd * 256 + (dc + 1) * 128, :]),
                        w=[("qF", hs, dc)], dma="qF%d%d" % (hs, dc))
                    P.op("sp", lambda e, hs=hs, dc=dc, hd=hd: e.dma_start(
                        out=kF[hs][:, dc, :], in_=qk_fm[1024 + hd * 256 + dc * 128:1024 + hd * 256 + (dc + 1) * 128, :]),
                        w=[("kF", hs, dc)], dma="kF%d%d" % (hs, dc))
                P.op("sp", lambda e, hs=hs, hd=hd: e.dma_start(
                    out=vA[hs][:, :, 0:256], in_=v_tm[:, hd * 256:(hd + 1) * 256].rearrange("(t p) c -> p t c", p=128)),
                    w=[("vA", hs)], dma="vA%d" % hs)
                P.op("sp", lambda e, hs=hs, hd=hd: e.dma_start(
                    out=moS[hs][:], in_=mo_tm[:, hd * 256:(hd + 1) * 256].rearrange("(t p) c -> p t c", p=128)),
                    w=[("moS", hs)], dma="moS%d" % hs)
                qkr = [("qF", hs, 0), ("qF", hs, 1), ("kF", hs, 0), ("kF", hs, 1)]

                def front(c, hs=hs, hd=hd, qkr=qkr):
                    cs = slice(c * 128, (c + 1) * 128)
                    p2 = c % 2
                    for dc in range(2):
                        P.op("pe", lambda e, dc=dc: e.matmul(
                            pbank[p2][:, 0:128], lhsT=kF[hs][:, dc, cs], rhs=qF[hs][:, dc, cs], start=(dc == 0), stop=(dc == 1)),
                            r=qkr, w=[PB(p2)])
                    if c < NT - 1:
                        kTv = pb16(4)[:, 0:256]
                        for dc in range(2):
                            P.op("pe", lambda e, dc=dc: e.transpose(
                                kTv[:, dc * 128:(dc + 1) * 128], kF[hs][:, dc, cs], ident_b[:]),
                                r=qkr + ["ident_b"], w=[PB(4)])
                    P.op("dve", lambda e: e.scalar_tensor_tensor(
                        out=SdT[p2][:], in0=pbank[p2][:, 0:128], scalar=e_s[:, c, hd:hd + 1], in1=tri_f[:],
                        op0=ALU.mult, op1=ALU.mult), r=[PB(p2), "e_s", "tri_f"], w=[("SdT", p2)])
                    if c < NT - 1:
                        P.op("act", lambda e: e.activation(
                            out=kw[p2][:], in_=kTv, func=AF.Copy, scale=e_s[:, c, hd:hd + 1]),
                            r=[PB(4), "e_s"], w=[("kw", p2)])

                front(0)
                for c in range(NT):
                    cs = slice(c * 128, (c + 1) * 128)
                    p2 = c % 2
                    if c + 1 < NT:
                        front(c + 1)
                    if c < NT - 1:
                        for dc in range(2):
                            P.op("pe", lambda e, p2=p2, dc=dc, hs=hs, c=c: e.matmul(
                                pbank[5 + dc][:, 0:257], lhsT=kw[p2][:, dc * 128:(dc + 1) * 128], rhs=vA[hs][:, c, :],
                                start=True, stop=True), r=[("kw", p2), ("vA", hs)], w=[PB(5 + dc)])
                        np2 = (c + 1) % 2
                        if c == 0:
                            for dc in range(2):
                                P.op("act", lambda e, dc=dc, c=c, hd=hd: e.activation(
                                    out=Cf[:, dc, :], in_=pbank[5 + dc][:, 0:257], func=AF.Copy, scale=ebL[:, c, hd:hd + 1]),
                                    r=[PB(5 + dc), "ebL"], w=["Cf"])
                        else:
                            for dc in range(2):
                                P.op("dve", lambda e, dc=dc: e.tensor_tensor(
                                    out=Cf[:, dc, :], in0=Cf[:, dc, :], in1=pbank[5 + dc][:, 0:257], op=ALU.add),
                                    r=[PB(5 + dc), "Cf"], w=["Cf"])
                            P.op("act", lambda e, c=c, hd=hd: e.activation(
                                out=Cf[:], in_=Cf[:], func=AF.Copy, scale=ebL[:, c, hd:hd + 1]), r=["Cf", "ebL"], w=["Cf"])
                        P.op("pool", lambda e, np2=np2: e.tensor_copy(out=Cb[np2][:], in_=Cf[:]), r=["Cf"], w=[("Cb", np2)])
                    pn = pbank[2 + p2][:, 0:257]
                    if c > 0:
                        for dc in range(2):
                            P.op("pe", lambda e, hs=hs, dc=dc, cs=cs, p2=p2, pn=pn: e.matmul(
                                pn, lhsT=qF[hs][:, dc, cs], rhs=Cb[p2][:, dc, :], start=(dc == 0), stop=False),
                                r=qkr + [("Cb", p2)], w=[PB(2 + p2)])
                    P.op("pe", lambda e, hs=hs, c=c, p2=p2, pn=pn: e.matmul(
                        pn, lhsT=SdT[p2][:], rhs=vA[hs][:, c, :], start=(c == 0), stop=True),
                        r=[("SdT", p2), ("vA", hs)], w=[PB(2 + p2)])
                    smp = sm[p2]
                    P.op("dve", lambda e, smp=smp, pn=pn, c=c, hd=hd: e.tensor_tensor(
                        out=smp[:, 6:7], in0=pn[:, 256:257], in1=ebi[:, c, hd:hd + 1], op=ALU.max),
                        r=[PB(2 + p2), "ebi"], w=[("sm6", p2)])
                    P.op("dve", lambda e, smp=smp, pn=pn: e.scalar_tensor_tensor(
                        out=smp[:, 0:1], in0=pn[:, 256:257], scalar=-1.0, in1=smp[:, 6:7], op0=ALU.mult, op1=ALU.max),
                        r=[PB(2 + p2), ("sm6", p2)], w=[("sm0", p2)])
                    P.op("dve", lambda e, smp=smp: e.reciprocal(out=smp[:, 1:2], in_=smp[:, 0:1]), r=[("sm0", p2)], w=[("sm1", p2)])
                    P.op("act", lambda e, smp=smp, pn=pn: e.activation(
                        out=junk2[:], in_=pn[:, 0:256], func=AF.Square, scale=smp[:, 1:2], accum_out=smp[:, 2:3]),
                        r=[PB(2 + p2), ("sm1", p2)], w=["junk2", ("sm2", p2)])
                    P.op("act", lambda e, smp=smp: e.activation(out=smp[:, 3:4], in_=smp[:, 2:3], func=AF.Sqrt, bias=epsc[:, 2:3]),
                         r=[("sm2", p2), "epsc"], w=[("sm3", p2)])
                    P.op("dve", lambda e, smp=smp: e.reciprocal(out=smp[:, 4:5], in_=smp[:, 3:4]), r=[("sm3", p2)], w=[("sm4", p2)])
                    P.op("dve", lambda e, smp=smp: e.scalar_tensor_tensor(
                        out=smp[:, 5:6], in0=smp[:, 4:5], scalar=16.0, in1=smp[:, 1:2], op0=ALU.mult, op1=ALU.mult),
                        r=[("sm4", p2), ("sm1", p2)], w=[("sm5", p2)])
                    P.op("dve", lambda e, smp=smp, pn=pn, p2=p2, hd=hd: e.scalar_tensor_tensor(
                        out=hn[p2][:], in0=pn[:, 0:256], scalar=smp[:, 5:6], in1=gnb[:, hd * 256:(hd + 1) * 256],
                        op0=ALU.mult, op1=ALU.mult), r=[PB(2 + p2), ("sm5", p2), "gnb"], w=[("hn", p2)])
                    P.op("pool", lambda e, p2=p2, c=c, hd=hd, hs=hs: e.tensor_tensor(
                        out=mix[:, c, hd * 256:(hd + 1) * 256], in0=hn[p2][:], in1=moS[hs][:, c, :], op=ALU.mult),
                        r=[("hn", p2), ("moS", hs)], w=[("mix", c)])
            P.barrier()
            if "mix" in dbg_outs and stop == "B":
                P.op("pool", lambda e: e.dma_start(out=dbg_outs["mix"], in_=mix[:]), r=[("mix", c) for c in range(NT)], dma="dbgm")
            if stop == "B":
                break

            AR.reset(MIXM)
            qn = [AR.alloc("qn", [128, S], BF16) for i in range(2)]
            kn = [AR.alloc("kn", [128, S], BF16) for i in range(2)]
            vA2 = [AR.alloc("vA2", [128, NT, 129], BF16) for i in range(2)]
            Mt = [AR.alloc("Mt", [128, MW], F32) for i in range(2)]
            kmf = [AR.alloc("kmf", [128, NB], F32) for i in range(2)]
            kmb = [AR.alloc("kmb", [128, NB], BF16) for i in range(2)]
            gm = [AR.alloc("gm", [128, NT, NB], F32) for i in range(2)]
            top8 = [AR.alloc("top8", [128, 8], F32) for i in range(2)]
            negm = [AR.alloc("negm", [128, NT, 8], BF16) for i in range(2)]
            negT = [AR.alloc("negT", [8, S], BF16) for i in range(2)]
            pex = [AR.alloc("pex", [128, 512], F32) for i in range(3)]
            PT = [AR.alloc("PT", [128, 512], BF16) for i in range(3)]
            rcp = [AR.alloc("rcp", [128, 4], F32) for i in range(2)]
            for i in range(2):
                P.op("pool", lambda e, i=i: e.memset(vA2[i][:, :, 128:129], 1.0), w=[("vA2", i)])
                if NB < 8:
                    P.op("pool", lambda e, i=i: e.memset(negm[i][:], 0.0), w=[("negm", i)])
                    P.op("pool", lambda e, i=i: e.memset(gm[i][:], 0.0), w=[("gm", i)])
            pa3 = pastadd[:].rearrange("p (t j) -> p t j", j=NB)
            pn3 = pastneg[:].rearrange("p (t j) -> p t j", j=NB)

            def prep_head(hd):
                hs = hd % 2
                P.op("sp", lambda e: e.dma_start(out=qn[hs][:], in_=aqk_fm[hd * 128:(hd + 1) * 128, :]),
                     w=[("qn", hs)], dma="qn%d" % hs)
                P.op("sp", lambda e: e.dma_start(out=kn[hs][:], in_=aqk_fm[1024 + hd * 128:1024 + (hd + 1) * 128, :]),
                     w=[("kn", hs)], dma="kn%d" % hs)
                P.op("sp", lambda e: e.dma_start(
                    out=vA2[hs][:, :, 0:128], in_=av_tm[:, hd * 128:(hd + 1) * 128].rearrange("(t p) c -> p t c", p=128)),
                    w=[("vA2", hs)], dma="vA2%d" % hs)
                P.op("sp", lambda e: e.dma_start(
                    out=Mt[hs][:], in_=bass.AP(d2.tensor, hd * 128 * W2 + 127, [[W2 - 1, 128], [1, MW]])),
                    w=[("Mt", hs)], dma="Mt%d" % hs)
                P.op("dve", lambda e: e.tensor_reduce(out=kmf[hs][:], in_=kn[hs][:].rearrange("p (j t) -> p j t", t=256),
                                                      axis=AX.X, op=ALU.add), r=[("kn", hs)], w=[("kmf", hs)])
                P.op("dve", lambda e: e.tensor_scalar(out=kmb[hs][:], in0=kmf[hs][:], scalar1=1.0 / 256.0, scalar2=None, op0=ALU.mult),
                     r=[("kmf", hs)], w=[("kmb", hs)])
                g_ps = pbank[7][:, 0:NT * NB].rearrange("p (t j) -> p t j", j=NB)
                for t in range(NT):
                    P.op("pe", lambda e, t=t: e.matmul(
                        g_ps[:, t, :], lhsT=qn[hs][:, t * 128:(t + 1) * 128], rhs=kmb[hs][:], start=True, stop=True),
                        r=[("qn", hs), ("kmb", hs)], w=[PB(7)])
                P.op("dve", lambda e: e.tensor_tensor(out=gm[hs][:, :, 0:NB], in0=g_ps, in1=pa3, op=ALU.add),
                     r=[PB(7), "pastadd"], w=[("gm", hs)])
                if NB >= 8:
                    for t in range(NT):
                        tp = t % 2
                        P.op("dve", lambda e, t=t, tp=tp: e.max(out=top8[tp][:], in_=gm[hs][:, t, :]), r=[("gm", hs)], w=[("top8", tp)])
                        P.op("dve", lambda e, t=t, tp=tp: e.scalar_tensor_tensor(
                            out=negm[hs][:, t, 0:NB], in0=gm[hs][:, t, :], scalar=top8[tp][:, 2:3], in1=pn3[:, t, :],
                            op0=ALU.is_lt, op1=ALU.mult), r=[("gm", hs), ("top8", tp), "pastneg"], w=[("negm", hs)])
                nT_ps = pb16(7)
                for t4 in range(NG):
                    for tt in range(4):
                        t = t4 * 4 + tt
                        P.op("pe", lambda e, t=t, tt=tt: e.transpose(
                            nT_ps[0:8, tt * 128:(tt + 1) * 128], negm[hs][:, t, :], ident_b[:]),
                            r=[("negm", hs), "ident_b"], w=[PB(7)])
                    P.op("act", lambda e, t4=t4: e.copy(out=negT[hs][:, t4 * 512:(t4 + 1) * 512], in_=nT_ps[0:8, 0:512]),
                         r=[PB(7)], w=[("negT", hs)])

            accb = [pbank[3 + qi][:, 0:129] for qi in range(4)]
            accres = [PB(3 + qi) for qi in range(4)]
            pairs = []
            for hd in range(8):
                for cq in range(NG):
                    nkt = min(4 * cq + 4, NT)
                    for kt in range(nkt):
                        pairs.append((hd, cq, kt, kt == nkt - 1))

            def stage1(i):
                hd, cq, kt, last = pairs[i]
                hs = hd % 2
                s3 = i % 3
                jb = kt // 2
                need_mask = (NB >= 8) and (2 * cq + 1 >= 4) and (jb < 2 * cq + 1)
                P.op("pe", lambda e: e.matmul(
                    pbank[s3][:], lhsT=kn[hs][:, kt * 128:(kt + 1) * 128], rhs=qn[hs][:, cq * 512:(cq + 1) * 512],
                    start=True, stop=not need_mask), r=[("kn", hs), ("qn", hs)], w=[PB(s3)])
                if need_mask:
                    P.op("pe", lambda e: e.matmul(
                        pbank[s3][:], lhsT=eall[:, jb * 128:(jb + 1) * 128], rhs=negT[hs][:, cq * 512:(cq + 1) * 512],
                        start=False, stop=True), r=["eall", ("negT", hs)], w=[PB(s3)])
                P.op("act", lambda e: e.activation(out=pex[s3][:], in_=pbank[s3][:], func=AF.Exp),
                     r=[PB(s3)], w=[("pex", s3)])
                j0 = cq * 512 - kt * 128 + 511
                P.op("dve", lambda e: e.tensor_tensor(
                    out=PT[s3][:], in0=pex[s3][:], in1=Mt[hs][:, j0:j0 + 512], op=ALU.mult),
                    r=[("pex", s3), ("Mt", hs)], w=[("PT", s3)])

            def stage2(i):
                hd, cq, kt, last = pairs[i]
                hs = hd % 2
                s3 = i % 3
                for qi in range(4):
                    t = 4 * cq + qi
                    if t < kt:
                        continue
                    P.op("pe", lambda e, qi=qi, t=t: e.matmul(
                        accb[qi], lhsT=PT[s3][:, qi * 128:(qi + 1) * 128], rhs=vA2[hs][:, kt, :],
                        start=(kt == 0), stop=(kt == t)), r=[("PT", s3), ("vA2", hs)], w=[accres[qi]])
                if last:
                    ap_ = (hd * NG + cq) % 2
                    for qi in range(4):
                        t = 4 * cq + qi
                        rc = rcp[ap_]
                        P.op("dve", lambda e, qi=qi, rc=rc: e.reciprocal(out=rc[:, qi:qi + 1], in_=accb[qi][:, 128:129]),
                             r=[accres[qi]], w=[("rcp", ap_, qi)])
                        P.op("act", lambda e, qi=qi, rc=rc, t=t: e.activation(
                            out=mix[:, t, 1024 + hd * 128:1024 + (hd + 1) * 128], in_=accb[qi][:, 0:128],
                            func=AF.Copy, scale=rc[:, qi:qi + 1]), r=[accres[qi], ("rcp", ap_, qi)], w=[("mix", t)])

            LOOK = 2
            npairs = len(pairs)
            per_head = npairs // 8
            if dbg.get("skip") == "C":
                npairs = -LOOK
            else:
                prep_head(0)
            for i in range(npairs + LOOK):
                if i < npairs:
                    hd_i = pairs[i][0]
                    if (i % per_head) == min(per_head // 2, per_head - 1) and hd_i + 1 < 8:
                        prep_head(hd_i + 1)
                    stage1(i)
                if i - LOOK >= 0:
                    stage2(i - LOOK)
            P.barrier()
            if "mix" in dbg_outs and stop == "C":
                P.op("pool", lambda e: e.dma_start(out=dbg_outs["mix"], in_=mix[:]), r=[("mix", c) for c in range(NT)], dma="dbgm")
            if stop == "C":
                break

            AR.reset(MIXM)
            mixT = AR.alloc("mixT", [128, KCH, S], BF16)
            wD = [AR.alloc("wD", [128, KCH, 512], BF16) for i in range(2)]
            xr = [AR.alloc("xr", [128, 512], F32) for i in range(3)]
            wo3 = w_out[l].rearrange("(k p) c -> p k c", p=128)
            load_wblock(wD, 0, wo3, 0, 512, "wD")
            tcn = 0
            for t in range(NT):
                for k4 in range(KCH // 4):
                    bk = 6 + tcn % 2
                    tcn += 1
                    pv = pb16(bk)[:, 0:512].rearrange("p (a b) -> p a b", b=128)
                    for kk in range(4):
                        k = k4 * 4 + kk
                        P.op("pe", lambda e, t=t, k=k, kk=kk, pv=pv: e.transpose(
                            pv[:, kk, :], mix[:, t, k * 128:(k + 1) * 128], ident_b[:]),
                            r=[("mix", t), "ident_b"], w=[PB(bk)])
                    eng = "act" if k4 % 2 == 0 else "dve"
                    if eng == "act":
                        P.op("act", lambda e, t=t, k4=k4, pv=pv: e.copy(out=mixT[:, k4 * 4:(k4 + 1) * 4, t * 128:(t + 1) * 128], in_=pv),
                             r=[PB(bk)], w=[("mixT", t)])
                    else:
                        P.op("dve", lambda e, t=t, k4=k4, pv=pv: e.tensor_copy(out=mixT[:, k4 * 4:(k4 + 1) * 4, t * 128:(t + 1) * 128], in_=pv),
                             r=[PB(bk)], w=[("mixT", t)])
            dcn = 0
            for cb in range(4):
                ws = cb % 2
                if cb + 1 < 4:
                    load_wblock(wD, cb + 1, wo3, (cb + 1) * 512, 512, "wD")
                wr = wres("wD", ws)
                for t in range(NT):
                    bk = dcn % 4
                    xs = dcn % 3
                    dcn += 1
                    P.op("sp", lambda e, xs=xs, t=t, cb=cb, xsrc=xsrc: e.dma_start(out=xr[xs][:], in_=xsrc[t * 128:(t + 1) * 128, cb * 512:(cb + 1) * 512]),
                         r=[("y", t, cb)], w=[("xr", xs)], dma="xr%d" % xs)
                    for k in range(KCH):
                        P.op("pe", lambda e, ws=ws, k=k, t=t, bk=bk: e.matmul(
                            pbank[bk][:], lhsT=mixT[:, k, t * 128:(t + 1) * 128], rhs=wD[ws][:, k, :],
                            start=(k == 0), stop=(k == KCH - 1)), r=wr + [("mixT", t)], w=[PB(bk)])
                    P.op("dve", lambda e, xs=xs, bk=bk: e.tensor_tensor(out=xr[xs][:], in0=xr[xs][:], in1=pbank[bk][:], op=ALU.add),
                         r=[PB(bk), ("xr", xs)], w=[("xr", xs)])
                    P.op("act", lambda e, xs=xs, t=t, cb=cb: e.dma_start(out=y[t * 128:(t + 1) * 128, cb * 512:(cb + 1) * 512], in_=xr[xs][:]),
                         r=[("xr", xs)], w=[("y", t, cb)], dma="xw%d" % xs)
            P.barrier()
            if stop == "D":
                break

            AR.reset(PH)
            h2T = AR.alloc("hT", [128, KCH, S], BF16)
            wu3 = w_up[l].rearrange("(k p) c -> p k c", p=128)
            wg = [AR.alloc("wg", [128, KCH, 512], BF16) for i in range(2)]
            wv = [AR.alloc("wv", [128, KCH, 512], BF16) for i in range(2)]
            load_wblock(wg, 0, wu3, 0, 512, "wg")
            load_wblock(wv, 0, wu3, DFF, 512, "wv")
            norm_transpose(y, norm_ffn, l, h2T, "n2")

            fw = AR.alloc("fw", [128, 3, 88], F32)
            fb = AR.alloc("fb", [128, 88], F32)
            ug = [AR.alloc("ug", [128, 2 + S], F32) for i in range(2)]
            uv = [AR.alloc("uv", [128, 2 + S], F32) for i in range(2)]
            cg = AR.alloc("cg", [128, S], F32)
            cv = AR.alloc("cv", [128, S], F32)
            sgl = AR.alloc("sgl", [128, S], F32)
            ast = [AR.alloc("ast", [128, S], BF16) for i in range(2)]
            for tap in range(3):
                P.op("sp", lambda e, tap=tap, l=l: e.dma_start(out=fw[:, tap, :], in_=conv_ffn_w[l][tap].rearrange("(c p) -> p c", p=128),
                                                           allow_slow_non_contiguous=True), w=["fw"], dma="fw")
            P.op("sp", lambda e, l=l: e.dma_start(out=fb[:], in_=conv_ffn_b[l].rearrange("(c p) -> p c", p=128),
                                              allow_slow_non_contiguous=True), w=["fb"], dma="fb")
            for i in range(2):
                P.op("pool", lambda e, i=i: e.memset(ug[i][:, 0:2], 0.0), w=[("ug", i)])
                P.op("pool", lambda e, i=i: e.memset(uv[i][:, 0:2], 0.0), w=[("uv", i)])
            NBLK = DFF // 512
            fcn = 0
            pcn = 0
            for bb in range(NBLK):
                ws = bb % 2
                if bb + 1 < NBLK:
                    load_wblock(wg, bb + 1, wu3, (bb + 1) * 512, 512, "wg")
                    load_wblock(wv, bb + 1, wu3, DFF + (bb + 1) * 512, 512, "wv")
                for j in range(4):
                    chg = bb * 4 + j
                    us = pcn % 2
                    pcn += 1
                    for (wt, wtag, ub, utag) in ((wg, "wg", ug, "ug"), (wv, "wv", uv, "uv")):
                        wr = wres(wtag, ws)
                        for n in range(NG):
                            bk = fcn % 8
                            fcn += 1
                            for k in range(KCH):
                                P.op("pe", lambda e, wt=wt, ws=ws, k=k, j=j, n=n, bk=bk: e.matmul(
                                    pbank[bk][:], lhsT=wt[ws][:, k, j * 128:(j + 1) * 128],
                                    rhs=h2T[:, k, n * 512:(n + 1) * 512], start=(k == 0), stop=(k == KCH - 1)),
                                    r=wr + ["hT"], w=[PB(bk)])
                            P.op("act", lambda e, ub=ub, us=us, n=n, bk=bk: e.copy(out=ub[us][:, 2 + n * 512:2 + (n + 1) * 512], in_=pbank[bk][:]),
                                 r=[PB(bk)], w=[(utag, us)])
                    P.op("dve", lambda e, us=us, chg=chg: e.tensor_scalar(
                        out=cg[:], in0=ug[us][:, 2:2 + S], scalar1=fw[:, 2, chg:chg + 1], scalar2=fb[:, chg:chg + 1],
                        op0=ALU.mult, op1=ALU.add), r=[("ug", us), "fw", "fb"], w=["cg"])
                    for tap in (1, 0):
                        P.op("dve", lambda e, us=us, chg=chg, tap=tap: e.scalar_tensor_tensor(
                            out=cg[:], in0=ug[us][:, tap:tap + S], scalar=fw[:, tap, chg:chg + 1], in1=cg[:],
                            op0=ALU.mult, op1=ALU.add), r=[("ug", us), "cg", "fw"], w=["cg"])
                    chv = 44 + chg
                    P.op("pool", lambda e, us=us, chv=chv: e.tensor_scalar(
                        out=cv[:], in0=uv[us][:, 2:2 + S], scalar1=fw[:, 2, chv:chv + 1], scalar2=fb[:, chv:chv + 1],
                        op0=ALU.mult, op1=ALU.add), r=[("uv", us), "fw", "fb"], w=["cv"])
                    for tap in (1, 0):
                        P.op("dve", lambda e, us=us, chv=chv, tap=tap: e.scalar_tensor_tensor(
                            out=cv[:], in0=uv[us][:, tap:tap + S], scalar=fw[:, tap, chv:chv + 1], in1=cv[:],
                            op0=ALU.mult, op1=ALU.add), r=[("uv", us), "cv", "fw"], w=["cv"])
                    P.op("act", lambda e: e.activation(out=sgl[:], in_=cg[:], func=AF.Silu), r=["cg"], w=["sgl"])
                    P.op("pool", lambda e, us=us: e.tensor_tensor(out=ast[us][:], in0=sgl[:], in1=cv[:], op=ALU.mult),
                         r=["sgl", "cv"], w=[("ast", us)])
                    P.op("sp", lambda e, us=us, chg=chg: e.dma_start(out=a_fm[chg * 128:(chg + 1) * 128, :], in_=ast[us][:]),
                         r=[("ast", us)], w=[("a_fm", chg)], dma="ast%d" % us)
            P.barrier()
            if stop == "F1":
                break

            AR.reset(PH)
            KF = DFF // 128
            wd = [AR.alloc("wd", [128, KF, 512], BF16) for i in range(2)]
            at = [AR.alloc("at", [128, KF, 512], BF16) for i in range(2)]
            xr2 = [AR.alloc("xr2", [128, 512], F32) for i in range(3)]
            wd3 = w_down[l].rearrange("(k p) c -> p k c", p=128)
            a3 = a_fm.rearrange("(k p) t -> p k t", p=128)
            load_wblock(wd, 0, wd3, 0, 512, "wd", kn=KF, nsplit=4)
            dcn = 0
            acn = 0
            for cb in range(4):
                ws = cb % 2
                if cb + 1 < 4:
                    load_wblock(wd, cb + 1, wd3, (cb + 1) * 512, 512, "wd", kn=KF, nsplit=4)
                wr = wres("wd", ws, 4)
                for tp in range(NT // 4):
                    as_ = acn % 2
                    acn += 1
                    for hh in range(2):
                        P.op("sp", lambda e, as_=as_, tp=tp, hh=hh: e.dma_start(
                            out=at[as_][:, hh * 22:(hh + 1) * 22, :], in_=a3[:, hh * 22:(hh + 1) * 22, tp * 512:(tp + 1) * 512]),
                            w=[("at", as_, hh)], dma="at%d_%d" % (as_, hh))
                    for tt in range(4):
                        t = tp * 4 + tt
                        bk = dcn % 4
                        xs = dcn % 3
                        dcn += 1
                        P.op("act", lambda e, xs=xs, t=t, cb=cb: e.dma_start(out=xr2[xs][:], in_=y[t * 128:(t + 1) * 128, cb * 512:(cb + 1) * 512]),
                             r=[("y", t, cb)], w=[("xr2", xs)], dma="xr2%d" % xs)
                        for k in range(KF):
                            P.op("pe", lambda e, ws=ws, k=k, tt=tt, as_=as_, bk=bk: e.matmul(
                                pbank[bk][:], lhsT=at[as_][:, k, tt * 128:(tt + 1) * 128], rhs=wd[ws][:, k, :],
                                start=(k == 0), stop=(k == KF - 1)), r=wr + [("at", as_, 0), ("at", as_, 1)], w=[PB(bk)])
                        P.op("dve", lambda e, xs=xs, bk=bk: e.tensor_tensor(out=xr2[xs][:], in0=xr2[xs][:], in1=pbank[bk][:], op=ALU.add),
                             r=[PB(bk), ("xr2", xs)], w=[("xr2", xs)])
                        P.op("act", lambda e, xs=xs, t=t, cb=cb: e.dma_start(out=y[t * 128:(t + 1) * 128, cb * 512:(cb + 1) * 512], in_=xr2[xs][:]),
                             r=[("xr2", xs)], w=[("y", t, cb)], dma="xw2%d" % xs)
            P.barrier()
        if "gates" in dbg_outs:
            P.op("sp", lambda e: e.dma_start(out=dbg_outs["gates"], in_=gates[:]), r=["gates"], dma="dbgg")
        P.barrier()
        P.emit()
    return nc


_WNAMES = ["norm_mix", "w_in", "gate_bias", "conv_qk_w", "conv_qk_b", "rel_bias", "w_out", "norm_ffn",
           "w_up", "conv_ffn_w", "conv_ffn_b", "w_down"]


PLACE = [0, 1, 4, 5]


def kernel(**inputs):
    x = np.asarray(inputs["x"], dtype=np.float32)
    B, S, _ = x.shape
    L = int(np.asarray(inputs["w_in"]).shape[0])
    nc = build_program(S=S, L=L)
    consts = host_consts(S)
    shared = {k: np.ascontiguousarray(np.asarray(inputs[k], dtype=np.float32)) for k in _WNAMES}
    shared["mlstm_norm"] = np.ascontiguousarray(np.asarray(inputs["mlstm_norm"], dtype=np.float32).reshape(L, 1024))
    shared["qk_norm"] = np.ascontiguousarray(np.asarray(inputs["qk_norm"], dtype=np.float32).reshape(L, 256))
    for k, v in consts.items():
        shared["c_" + k] = v
    ncores = 8 if B == 4 else B
    place = PLACE if B == 4 else list(range(B))
    zero_x = np.zeros((S, x.shape[2]), np.float32)
    pad = dict(shared)
    for k in ("w_in", "w_out", "w_up", "w_down", "conv_qk_b", "conv_ffn_b"):
        pad[k] = np.zeros_like(shared[k])
    pad["x"] = zero_x
    in_maps = []
    for c in range(ncores):
        if c in place:
            m = dict(shared)
            m["x"] = np.ascontiguousarray(x[place.index(c)])
        else:
            m = pad
        in_maps.append(m)
    res = run_bass_kernel_spmd(nc, in_maps, core_ids=list(range(ncores)))
    return np.stack([np.asarray(res.results[place[b]]["y"], dtype=np.float32) for b in range(B)], axis=0)
```
